# Optimizing a Trainium2 kernel written in Bass

```python
import math
import jax, jax.numpy as jnp
from jax import lax
import numpy as np

D_MODEL = 1024
BATCH = 4
SEQ = 8192
DEPTH = 4

GRID_W = 64
CTX_LEN = 256
N_BRANCH = 4
BRANCH_W = D_MODEL // 2
CONV_A_K = 3
CONF_K = 31
DIFF_HEADS = 4
DIFF_D = D_MODEL // 16
DIFF_V = 2 * DIFF_D
NA_HEADS = 8
NA_D = D_MODEL // 16
NA_WIN_R = 8
NA_WIN_C = 16
D_FF = 4 * D_MODEL
Q_BLOCK = 128
ROPE_BASE = 10000.0
LN_EPS = 1e-5
DEEPNORM_ALPHA = (2 * DEPTH) ** 0.25
DEEPNORM_BETA = (8 * DEPTH) ** -0.25
NEG_INF = -1e30

IN_WIDTHS = (BRANCH_W, BRANCH_W, BRANCH_W,
             BRANCH_W, BRANCH_W,
             DIFF_HEADS * 2 * DIFF_D, DIFF_HEADS * 2 * DIFF_D, DIFF_HEADS * DIFF_V,
             NA_HEADS * NA_D, NA_HEADS * NA_D, NA_HEADS * NA_D,
             N_BRANCH * D_MODEL)
IN_OFFSETS = tuple(sum(IN_WIDTHS[:i]) for i in range(len(IN_WIDTHS)))
D_IN = sum(IN_WIDTHS)
I_AB, I_AC, I_AX, I_CA, I_CG, I_DQ, I_DK, I_DV, I_NQ, I_NK, I_NV, I_GATE = range(12)

kernel_name = 'hybrid_parallel_diffusion_trunk'


def layer_norm(x, g, b):
    xf = x.astype(jnp.float32)
    mu = jnp.mean(xf, axis=-1, keepdims=True)
    var = jnp.mean(jnp.square(xf - mu), axis=-1, keepdims=True)
    return ((xf - mu) * lax.rsqrt(var + LN_EPS)).astype(x.dtype) * g + b


def rms_norm(x, g):
    xf = x.astype(jnp.float32)
    return (xf * lax.rsqrt(jnp.mean(xf * xf, axis=-1, keepdims=True) + LN_EPS)).astype(x.dtype) * g


def dwconv(x, w, b):
    k = w.shape[0]
    y = lax.conv_general_dilated(x, w[:, None, :], window_strides=(1,), padding=((k // 2, k // 2),),
                                 dimension_numbers=('NWC', 'WIO', 'NWC'), feature_group_count=x.shape[-1])
    return y + b


def ada_mod(cond, w, b):
    return jnp.split(jax.nn.silu(cond) @ w + b, 6, axis=-1)


def in_cols(u, w, b, i):
    lo = IN_OFFSETS[i]
    hi = lo + IN_WIDTHS[i]
    return u @ w[:, lo:hi] + b[lo:hi]


def split_proj(proj):
    return jnp.split(proj, list(IN_OFFSETS[1:]), axis=-1)


def axial_rope_tables(n):
    t = jnp.arange(n)
    nf = DIFF_D // 4
    freqs = jnp.power(ROPE_BASE, -jnp.arange(nf, dtype=jnp.float32) / nf)
    pos = jnp.stack([t // GRID_W, t % GRID_W], axis=-1).astype(jnp.float32)
    ang = pos[:, :, None] * freqs
    return jnp.cos(ang), jnp.sin(ang)


def axial_rope(x, cos, sin):
    xs = x.reshape(x.shape[:-1] + (2, 2, DIFF_D // 4))
    x1, x2 = xs[..., 0, :], xs[..., 1, :]
    cs = cos[:, None, None].astype(x.dtype)
    sn = sin[:, None, None].astype(x.dtype)
    out = jnp.stack([x1 * cs - x2 * sn, x2 * cs + x1 * sn], axis=-2)
    return out.reshape(x.shape)


def local_branches(p, cw, cb, dw, db, lg, lb):
    y_a = p[I_AB] * dwconv(p[I_AC] * p[I_AX], cw, cb)
    glu = p[I_CA] * jax.nn.sigmoid(p[I_CG])
    y_b = jax.nn.silu(layer_norm(dwconv(glu, dw, db), lg, lb))
    return y_a, y_b


def diff_core(q, k, v, lam):
    s = jnp.einsum('bqhmd,bkhmd->bhmqk', q, k).astype(jnp.float32)
    p = jax.nn.softmax(s, axis=-1)
    pd = (p[:, :, 0] - lam * p[:, :, 1]).astype(v.dtype)
    return jnp.einsum('bhqk,bkhv->bqhv', pd, v)


def diff_latent(q, k, v, k_ctx, v_ctx, lam):
    B, S, H, M, d = q.shape
    k_all = jnp.concatenate([k_ctx, k], axis=1)
    v_all = jnp.concatenate([v_ctx, v], axis=1)
    qb = q.reshape(B, S // Q_BLOCK, Q_BLOCK, H, M, d).swapaxes(0, 1)
    o = lax.map(lambda qi: diff_core(qi, k_all, v_all, lam), qb)
    return o.swapaxes(0, 1).reshape(B, S, H, v.shape[-1])


def diff_post(o, g, lam_init):
    B, T = o.shape[:2]
    return (rms_norm(o, g) * (1.0 - lam_init)).reshape(B, T, -1)


def dense_attn(q, k, v):
    B, T = q.shape[:2]
    s = jnp.einsum('bqhd,bkhd->bhqk', q, k).astype(jnp.float32)
    p = jax.nn.softmax(s, axis=-1).astype(v.dtype)
    return jnp.einsum('bhqk,bkhd->bqhd', p, v).reshape(B, T, -1)


def na_latent(q, k, v, k_ctx, v_ctx, rpb):
    B, S, H, d = q.shape
    rows = S // GRID_W
    win_r = min(NA_WIN_R, rows)
    kg = k.reshape(B, rows, GRID_W, H, d)
    vg = v.reshape(B, rows, GRID_W, H, d)
    qg = q.reshape(B, rows, GRID_W, H, d).swapaxes(0, 1)
    cidx = jnp.arange(GRID_W)
    cstart = jnp.clip(cidx - NA_WIN_C // 2, 0, GRID_W - NA_WIN_C)
    col_ok = (cidx[None, :] >= cstart[:, None]) & (cidx[None, :] < cstart[:, None] + NA_WIN_C)
    dc_idx = jnp.clip(cidx[None, :] - cidx[:, None] + NA_WIN_C - 1, 0, 2 * NA_WIN_C - 2)
    rpb_c = rpb[:, :, dc_idx]
    n_lat = win_r * GRID_W

    def row_fn(args):
        qr, r = args
        r0 = jnp.clip(r - win_r // 2, 0, rows - win_r)
        kb = lax.dynamic_slice_in_dim(kg, r0, win_r, axis=1)
        vb = lax.dynamic_slice_in_dim(vg, r0, win_r, axis=1)
        dr_idx = r0 + jnp.arange(win_r) - r + NA_WIN_R - 1
        bias = rpb_c[:, dr_idx].transpose(0, 2, 1, 3)[None].astype(jnp.float32)
        s_lat = jnp.einsum('bqhd,bwkhd->bhqwk', qr, kb).astype(jnp.float32) + bias
        s_lat = jnp.where(col_ok[:, None, :], s_lat, NEG_INF)
        s_ctx = jnp.einsum('bqhd,bkhd->bhqk', qr, k_ctx).astype(jnp.float32)
        s = jnp.concatenate([s_lat.reshape(B, H, GRID_W, n_lat), s_ctx], axis=-1)
        p = jax.nn.softmax(s, axis=-1).astype(v.dtype)
        p_lat = p[..., :n_lat].reshape(B, H, GRID_W, win_r, GRID_W)
        return (jnp.einsum('bhqwk,bwkhd->bqhd', p_lat, vb)
                + jnp.einsum('bhqk,bkhd->bqhd', p[..., n_lat:], v_ctx))

    o = lax.map(row_fn, (qg, jnp.arange(rows)))
    return o.swapaxes(0, 1).reshape(B, S, H * d)


def merge_out(gates, ys, wb, bb, wo):
    gs = jnp.split(gates, N_BRANCH, axis=-1)
    merged = jax.nn.sigmoid(gs[0]) * (ys[0] @ wb[0] + bb[0])
    for i in range(1, N_BRANCH):
        merged = merged + jax.nn.sigmoid(gs[i]) * (ys[i] @ wb[i] + bb[i])
    return merged @ wo


def sq_relu_ffn(h, w1, w2):
    a = jax.nn.relu(h @ w1)
    return (a * a) @ w2


def setup_inputs(seed: int = 0) -> dict:
    key = jax.random.key(seed)
    ks = jax.random.split(key, 24)
    nrm = lambda k, shape, s: jax.random.normal(k, shape, jnp.float32) * s
    gate_offset = jnp.repeat(jnp.array([0.0, 0.0, 1.0, 0.0, 0.0, 1.0], jnp.float32), D_MODEL)
    return {
        'x': nrm(ks[0], (BATCH, SEQ, D_MODEL), 1.0),
        'c': nrm(ks[1], (BATCH, D_MODEL), 1.0),
        'ctx': nrm(ks[2], (BATCH, CTX_LEN, D_MODEL), 1.0),
        'c_ctx': nrm(ks[3], (D_MODEL,), 1.0),
        'w_ada': nrm(ks[4], (DEPTH, D_MODEL, 6 * D_MODEL), 0.3 * D_MODEL ** -0.5),
        'b_ada': nrm(ks[5], (DEPTH, 6 * D_MODEL), 0.02) + gate_offset,
        'w_in': nrm(ks[6], (DEPTH, D_MODEL, D_IN), D_MODEL ** -0.5),
        'b_in': nrm(ks[7], (DEPTH, D_IN), 0.02),
        'conv_a_w': nrm(ks[8], (DEPTH, CONV_A_K, BRANCH_W), CONV_A_K ** -0.5),
        'conv_a_b': nrm(ks[9], (DEPTH, BRANCH_W), 0.02),
        'conf_dw_w': nrm(ks[10], (DEPTH, CONF_K, BRANCH_W), CONF_K ** -0.5),
        'conf_dw_b': nrm(ks[11], (DEPTH, BRANCH_W), 0.02),
        'conf_ln_g': 1.0 + nrm(ks[12], (DEPTH, BRANCH_W), 0.02),
        'conf_ln_b': nrm(ks[13], (DEPTH, BRANCH_W), 0.02),
        'diff_lambda': nrm(ks[14], (DEPTH, 4, DIFF_D), 0.1),
        'diff_norm_g': 1.0 + nrm(ks[15], (DEPTH, DIFF_V), 0.02),
        'na_rpb': nrm(ks[16], (DEPTH, NA_HEADS, 2 * NA_WIN_R - 1, 2 * NA_WIN_C - 1), 0.02),
        'w_branch': nrm(ks[17], (DEPTH, N_BRANCH, BRANCH_W, D_MODEL), BRANCH_W ** -0.5),
        'b_branch': nrm(ks[18], (DEPTH, N_BRANCH, D_MODEL), 0.02),
        'w_o': nrm(ks[19], (DEPTH, D_MODEL, D_MODEL), DEEPNORM_BETA * D_MODEL ** -0.5),
        'ln_g': 1.0 + nrm(ks[20], (DEPTH, 2, D_MODEL), 0.02),
        'ln_b': nrm(ks[21], (DEPTH, 2, D_MODEL), 0.02),
        'w_ff1': nrm(ks[22], (DEPTH, D_MODEL, D_FF), D_MODEL ** -0.5),
        'w_ff2': nrm(ks[23], (DEPTH, D_FF, D_MODEL), DEEPNORM_BETA * D_FF ** -0.5),
    }


def reference(x, c, ctx, c_ctx, w_ada, b_ada, w_in, b_in, conv_a_w, conv_a_b, conf_dw_w, conf_dw_b,
              conf_ln_g, conf_ln_b, diff_lambda, diff_norm_g, na_rpb, w_branch, b_branch, w_o,
              ln_g, ln_b, w_ff1, w_ff2):
    B, S, _ = x.shape
    rope_cos, rope_sin = axial_rope_tables(S)
    xc = ctx
    for l in range(DEPTH):
        last = l == DEPTH - 1
        L = xc.shape[1]
        lam_init = 0.8 - 0.6 * math.exp(-0.3 * l)
        lq1, lk1, lq2, lk2 = diff_lambda[l].astype(jnp.float32)
        lam = jnp.exp(jnp.sum(lq1 * lk1)) - jnp.exp(jnp.sum(lq2 * lk2)) + lam_init
        sh1, sc1, g1, sh2, sc2, g2 = ada_mod(c[:, None, :], w_ada[l], b_ada[l])
        csh1, csc1, cg1, csh2, csc2, cg2 = ada_mod(c_ctx, w_ada[l], b_ada[l])

        u_c = xc * (1.0 + csc1) + csh1
        if last:
            dk_c, dv_c, nk_c, nv_c = (in_cols(u_c, w_in[l], b_in[l], i) for i in (I_DK, I_DV, I_NK, I_NV))
        else:
            pc = split_proj(u_c @ w_in[l] + b_in[l])
            dk_c, dv_c, nk_c, nv_c = pc[I_DK], pc[I_DV], pc[I_NK], pc[I_NV]
        k_dc = dk_c.reshape(B, L, DIFF_HEADS, 2, DIFF_D)
        v_dc = dv_c.reshape(B, L, DIFF_HEADS, DIFF_V)
        k_nc = nk_c.reshape(B, L, NA_HEADS, NA_D)
        v_nc = nv_c.reshape(B, L, NA_HEADS, NA_D)

        u = x * (1.0 + sc1) + sh1
        p = split_proj(u @ w_in[l] + b_in[l])
        y_a, y_b = local_branches(p, conv_a_w[l], conv_a_b[l], conf_dw_w[l], conf_dw_b[l],
                                  conf_ln_g[l], conf_ln_b[l])
        q_d = axial_rope(p[I_DQ].reshape(B, S, DIFF_HEADS, 2, DIFF_D), rope_cos, rope_sin) * DIFF_D ** -0.5
        k_d = axial_rope(p[I_DK].reshape(B, S, DIFF_HEADS, 2, DIFF_D), rope_cos, rope_sin)
        v_d = p[I_DV].reshape(B, S, DIFF_HEADS, DIFF_V)
        y_c = diff_post(diff_latent(q_d, k_d, v_d, k_dc, v_dc, lam), diff_norm_g[l], lam_init)
        q_n = p[I_NQ].reshape(B, S, NA_HEADS, NA_D) * NA_D ** -0.5
        k_n = p[I_NK].reshape(B, S, NA_HEADS, NA_D)
        v_n = p[I_NV].reshape(B, S, NA_HEADS, NA_D)
        y_d = na_latent(q_n, k_n, v_n, k_nc, v_nc, na_rpb[l])
        y = merge_out(p[I_GATE], (y_a, y_b, y_c, y_d), w_branch[l], b_branch[l], w_o[l])
        x = layer_norm(DEEPNORM_ALPHA * x + g1 * y, ln_g[l, 0], ln_b[l, 0])
        x = layer_norm(DEEPNORM_ALPHA * x + g2 * sq_relu_ffn(x * (1.0 + sc2) + sh2, w_ff1[l], w_ff2[l]),
                       ln_g[l, 1], ln_b[l, 1])

        if not last:
            y_ac, y_bc = local_branches(pc, conv_a_w[l], conv_a_b[l], conf_dw_w[l], conf_dw_b[l],
                                        conf_ln_g[l], conf_ln_b[l])
            q_dc = pc[I_DQ].reshape(B, L, DIFF_HEADS, 2, DIFF_D) * DIFF_D ** -0.5
            y_cc = diff_post(diff_core(q_dc, k_dc, v_dc, lam), diff_norm_g[l], lam_init)
            q_nc = pc[I_NQ].reshape(B, L, NA_HEADS, NA_D) * NA_D ** -0.5
            y_dc = dense_attn(q_nc, k_nc, v_nc)
            yc = merge_out(pc[I_GATE], (y_ac, y_bc, y_cc, y_dc), w_branch[l], b_branch[l], w_o[l])
            xc = layer_norm(DEEPNORM_ALPHA * xc + cg1 * yc, ln_g[l, 0], ln_b[l, 0])
            xc = layer_norm(DEEPNORM_ALPHA * xc + cg2 * sq_relu_ffn(xc * (1.0 + csc2) + csh2, w_ff1[l], w_ff2[l]),
                            ln_g[l, 1], ln_b[l, 1])
    return x
```

```python
import numpy as np
import concourse.bass as bass
import concourse.mybir as mybir
from concourse.bass_utils import run_bass_kernel_spmd
from contextlib import ExitStack

F32 = mybir.dt.float32
BF16 = mybir.dt.bfloat16
AF = mybir.ActivationFunctionType
ALU = mybir.AluOpType
AX = mybir.AxisListType

ENGS = ['pe', 'act', 'dve', 'pool', 'sp']
DMAQ = ('sp', 'pool', 'act')
EPOCH = 12000
NDSEM = {'sp': 40, 'pool': 24, 'act': 8}


class Buf:
    __slots__ = ('name', 'w', 'r', 'prev', 'open')

    def __init__(self, name):
        self.name = name
        self.w = {}
        self.r = {}
        self.prev = set()
        self.open = False


class Op:
    __slots__ = ('eng', 'fn', 'deps', 'dma', 'sig', 'dsem', 'dval', 'signals', 'cc')

    def __init__(self, eng, fn, deps, dma):
        self.eng = eng
        self.fn = fn
        self.deps = deps
        self.dma = dma
        self.sig = None
        self.dsem = None
        self.dval = None
        self.signals = False
        self.cc = False


class Prog:
    def __init__(self, nc):
        self.nc = nc
        self.ops = []
        self.eng_ops = {e: [] for e in ENGS}
        self.dcount = {q: 0 for q in DMAQ}
        self.dlast = {q: {} for q in DMAQ}
        self.final_dmas = []
        self.ncc = 0

    def buf(self, name):
        return Buf(name)

    def _key(self, oid):
        o = self.ops[oid]
        return ('d', oid) if o.dma else o.eng

    def op(self, eng, fn, r=(), w=(), wacc=(), dma=False, cc=False):
        oid = len(self.ops)
        deps = set()
        for b in r:
            deps.update(b.w.values())
        for b in w:
            deps.update(b.w.values())
            deps.update(b.r.values())
        for b in wacc:
            if b.open:
                deps.update(b.prev)
            else:
                deps.update(b.w.values())
                deps.update(b.r.values())
        o = Op(eng, fn, deps, dma)
        if cc:
            o.cc = True
            o.dma = True
            self.ncc += 1
            o.dsem = 'cc'
            o.dval = self.ncc
            self.last_cc = oid
        elif dma:
            q = eng
            slot = self.dcount[q] % NDSEM[q]
            o.dsem = slot
            o.dval = 16 * (self.dcount[q] // NDSEM[q] + 1)
            if slot in self.dlast[q]:
                deps.add(self.dlast[q][slot])
            self.dlast[q][slot] = oid
            self.dcount[q] += 1
        self.ops.append(o)
        self.eng_ops[eng].append(oid)
        key = ('d', oid) if (dma or cc) else eng
        for b in r:
            b.r[key] = oid
            b.open = False
        for b in w:
            b.w = {key: oid}
            b.r = {}
            b.open = False
            b.prev = set()
        for b in wacc:
            if b.open:
                b.w[key] = oid
            else:
                b.prev = set(b.w.values()) | set(b.r.values())
                b.w = {key: oid}
                b.r = {}
                b.open = True
        return oid

    def pe(self, fn, r=(), w=(), wacc=()):
        return self.op('pe', fn, r, w, wacc)

    def act(self, fn, r=(), w=(), wacc=()):
        return self.op('act', fn, r, w, wacc)

    def dve(self, fn, r=(), w=(), wacc=()):
        return self.op('dve', fn, r, w, wacc)

    def pool(self, fn, r=(), w=(), wacc=()):
        return self.op('pool', fn, r, w, wacc)

    def dma(self, q, out, in_, r=(), w=(), wacc=(), final=False, **kw):
        oid = self.op(q, lambda e: e.dma_start(out=out, in_=in_, **kw), r, w, wacc, dma=True)
        if final:
            self.final_dmas.append(oid)
        return oid

    def emit(self, stack):
        nc = self.nc
        ops = self.ops
        for o in ops:
            for d in o.deps:
                do = ops[d]
                if do.dma:
                    continue
                if do.eng == o.eng and o.eng == 'pe':
                    continue
                do.signals = True
        nsig = {e: 0 for e in ENGS}
        for e in ENGS:
            for oid in self.eng_ops[e]:
                o = ops[oid]
                if (not o.dma) and o.signals:
                    nsig[e] += 1
                    o.sig = nsig[e]
        esems = {}
        for e in ENGS:
            n = (nsig[e] + EPOCH - 1) // EPOCH
            esems[e] = [stack.enter_context(nc.semaphore(f"s_{e}_{i}")) for i in range(max(n, 1))]
        dsems = {q: [stack.enter_context(nc.semaphore(f"d_{q}_{i}")) for i in range(min(NDSEM[q], self.dcount[q]))]
                 for q in DMAQ}
        self.nsig = nsig
        ccsem = stack.enter_context(nc.semaphore('cc_sem')) if self.ncc else None
        handles = {'pe': 'tensor', 'act': 'scalar', 'dve': 'vector', 'pool': 'gpsimd', 'sp': 'sync'}

        def emit_eng(ename, e):
            seen = {}
            for oid in self.eng_ops[ename]:
                o = ops[oid]
                need = {}
                for d in o.deps:
                    do = ops[d]
                    if do.cc:
                        k = ('c',)
                        v = do.dval
                    elif do.dma:
                        k = ('d', do.eng, do.dsem)
                        v = do.dval
                    else:
                        if do.eng == ename and ename == 'pe':
                            continue
                        k = ('e', do.eng)
                        v = do.sig
                    if v > need.get(k, 0):
                        need[k] = v
                for k, v in need.items():
                    if seen.get(k, 0) >= v:
                        continue
                    seen[k] = v
                    if k[0] == 'c':
                        e.wait_ge(ccsem, v)
                    elif k[0] == 'd':
                        e.wait_ge(dsems[k[1]][k[2]], v)
                    else:
                        ep = (v - 1) // EPOCH
                        e.wait_ge(esems[k[1]][ep], v - ep * EPOCH)
                ins = o.fn(e)
                if o.cc:
                    ins.then_inc(ccsem)
                elif o.dma:
                    ins.then_inc(dsems[ename][o.dsem], 16)
                elif o.signals:
                    ep = (o.sig - 1) // EPOCH
                    ins.then_inc(esems[ename][ep], 1)
            if ename == 'sp':
                for oid in self.final_dmas:
                    do = ops[oid]
                    e.wait_ge(dsems[do.eng][do.dsem], do.dval)

        block = stack.enter_context(nc.Block())
        for ename in ENGS:
            dec = getattr(block, handles[ename])

            def mk(ename):
                def body(e):
                    emit_eng(ename, e)
                return body
            dec(mk(ename))

DM = 1024
SOWN = 4096
CTXL = 256
NTOK = SOWN + CTXL
ALPHA = 8.0 ** 0.25
LN_EPS = 1e-5
NEG = -1.0e4
V_BIN, V_BADA, V_CW, V_CB, V_DW, V_DB, V_LG, V_LB, V_DG, V_BB, V_LAM = 0, 76, 124, 136, 140, 264, 268, 272, 276, 277, 309
NVEC = 311
R_BDV, R_BNV, R_BG1, R_BG2, R_LNG0, R_LNB0, R_LNG1, R_LNB1, R_LAM = 0, 512, 1024, 2048, 3072, 4096, 5120, 6144, 7168
NROWS = 7424
ARENA_WORDS = 51200


class T:
    __slots__ = ('ap', 'b')

    def __init__(self, ap, b):
        self.ap = ap
        self.b = b

    def __getitem__(self, k):
        return self.ap[k]


class Arena:
    def __init__(self, ar):
        self.ar = ar
        self.off = 0

    def alloc(self, name, free, dtype, parts=128):
        n = 1
        for f in free:
            n *= f
        words = n if dtype == F32 else (n + 1) // 2
        words = (words + 7) // 8 * 8
        assert self.off + words <= ARENA_WORDS, (name, self.off, words)
        ap = self.ar[0:parts, self.off:self.off + words]
        if dtype != F32:
            ap = ap.bitcast(dtype)
        ap = ap[:, 0:n]
        if len(free) == 2:
            ap = ap.rearrange("p (a b) -> p a b", a=free[0])
        elif len(free) == 3:
            ap = ap.rearrange("p (a b c) -> p a b c", a=free[0], b=free[1])
        self.off += words
        return T(ap, Buf(name))

    def ring(self, name, n, free, dtype, parts=128):
        return Ring([self.alloc(f"{name}{i}", free, dtype, parts) for i in range(n)])


class Ring:
    def __init__(self, items):
        self.items = items
        self.i = 0

    def next(self):
        t = self.items[self.i % len(self.items)]
        self.i += 1
        return t


def ACT(P, out, in_, func, r, w=(), wacc=(), **kw):
    return P.op('act', lambda e: e.activation(out=out, in_=in_, func=func, **kw), r, w, wacc)


def TT(P, eng, out, in0, in1, op, r, w=(), wacc=()):
    return P.op(eng, lambda e: e.tensor_tensor(out=out, in0=in0, in1=in1, op=op), r, w, wacc)


def TS(P, eng, out, in0, s1, s2, op0, op1, r, w=(), wacc=()):
    if op1 is None:
        return P.op(eng, lambda e: e.tensor_scalar(out=out, in0=in0, scalar1=s1, scalar2=None, op0=op0), r, w, wacc)
    return P.op(eng, lambda e: e.tensor_scalar(out=out, in0=in0, scalar1=s1, scalar2=s2, op0=op0, op1=op1), r, w, wacc)


def STT(P, eng, out, in0, scalar, in1, op0, op1, r, w=(), wacc=()):
    return P.op(eng, lambda e: e.scalar_tensor_tensor(out=out, in0=in0, scalar=scalar, in1=in1, op0=op0, op1=op1), r, w, wacc)


def MM(P, bank, out, lhsT, rhs, start, stop, r):
    fn = lambda e: e.matmul(out, lhsT, rhs, start=start, stop=stop)
    if start:
        return P.op('pe', fn, r, w=[bank.b])
    return P.op('pe', fn, r, wacc=[bank.b])


def TR(P, bank, out, in_, ident, r):
    return P.op('pe', lambda e: e.transpose(out, in_, ident), r, w=[bank.b])


def load_w(P, dst, src, k_chunks, c0, c1, wbuf, q='pool'):
    for k in range(k_chunks):
        for a in range(c0, c1, 2048):
            b = min(c1, a + 2048)
            P.dma(q, dst[:, k, a - c0:b - c0], src[k * 128:(k + 1) * 128, a:b], wacc=[wbuf])


class Ctx:
    pass


def build_program(n_layers, fused):
    nc = bass.Bass("TRN2", target_bir_lowering=False)
    C = Ctx()
    C.nc = nc
    dt = nc.dram_tensor

    def din(name, shape, dtype=F32):
        return dt(name, list(shape), dtype, kind="ExternalInput").ap()

    def dsc(name, shape, dtype=BF16):
        return dt(name, list(shape), dtype, kind="Internal").ap()

    L = n_layers
    I = Ctx()
    I.x_own = din("x_own", [SOWN, DM])
    I.x_seq = din("x_seq", [2 * SOWN, DM])
    I.xc = din("xc", [CTXL, DM])
    I.cc = din("cc", [128, 8, 2])
    I.w_ada = din("w_ada", [L, DM, 6 * DM])
    I.w_in = din("w_in", [L, DM, 9728])
    I.vec = din("vec", [L, 128, NVEC])
    I.rows = din("rows", [L, 1, NROWS])
    I.rpbT = din("rpbT", [L, 32, 120])
    I.w_branch = din("w_branch", [L, 2048, DM])
    I.w_o = din("w_o", [L, DM, DM])
    I.w_ff1 = din("w_ff1", [L, DM, 4 * DM])
    I.w_ff2 = din("w_ff2", [L, 4 * DM, DM])
    I.rope_own = din("rope_own", [128, 2, SOWN])
    I.rope_seq = din("rope_seq", [128, 2, 2 * SOWN])
    I.halo_flag = din("halo_flag", [128, 2])
    I.perm = din("perm", [128, 128])
    I.oh = din("oh", [32, 4096])
    I.a16 = din("a16", [16, 8 * 128])
    I.rm = din("rm", [16, 8 * 512])
    O = Ctx()
    O.x_out = dt("x_out", [SOWN, DM], F32, kind="ExternalOutput").ap()
    O.xc_out = dt("xc_out", [CTXL, DM], F32, kind="ExternalOutput").ap()
    S = Ctx()
    S.UT = dsc("s_ut", [128, 8, NTOK])
    S.CONV = dsc("s_conv", [3, 128, 4, SOWN + 32])
    S.CONVC = dsc("s_convc", [3, 128, 4, CTXL + 32])
    S.QD = dsc("s_qd", [128, 4, NTOK])
    S.KD = dsc("s_kd", [128, 4, 8448])
    S.VD = dsc("s_vd", [8448, 512])
    S.QN = dsc("s_qn", [128, 4, NTOK])
    S.KN = dsc("s_kn", [128, 4, SOWN + 512])
    S.KNC = dsc("s_knc", [128, 4, CTXL])
    S.VN = dsc("s_vn", [SOWN + 512, 512])
    S.VNC = dsc("s_vnc", [CTXL, 512])
    S.Y = dsc("s_y", [128, 16, NTOK])
    S.X1 = dsc("s_x1", [NTOK, DM], F32)
    S.RPBF = dsc("s_rpbf", [8, 15, 64, 64])
    S.GB = dsc("s_gb", [4, 128, DM], F32)
    S.XC = dsc("s_xc", [2, CTXL, DM], F32)
    S.AIN = [dt(f"cc_in{i}", [512, DM], F32, kind="Internal") for i in range(8)] if n_layers > 1 else None
    S.STG = [dt(f"cc_out{i}", [1024, DM], F32, kind="Internal") for i in range(8)] if n_layers > 1 else None
    C.I, C.O, C.S = I, O, S

    with ExitStack() as st:
        P = Prog(nc)
        C.P = P
        ar = st.enter_context(nc.sbuf_tensor("arena", [128, ARENA_WORDS], F32))
        A = Arena(ar)
        C.A = A
        C.ps = [T(st.enter_context(nc.psum_tensor(f"ps{i}", [128, 512], F32))[:, :], Buf(f"ps{i}")) for i in range(8)]
        C.dummy = Buf("dummy")
        setup_persistent(C)
        base = A.off
        C.ain_b = [Buf(f"ain{i}") for i in range(8)]
        C.stg_b = [Buf(f"stg{i}") for i in range(8)]
        for l in range(n_layers):
            C.l = l
            last_prog_layer = (l == n_layers - 1)
            C.final = last_prog_layer
            if l == 0:
                C.own_rows = lambda t0, n: I.x_own[t0:t0 + n, :]
                C.glob_tile = lambda gi: I.x_seq[gi * 512:(gi + 1) * 512, :]
                C.xin_ctx = I.xc
            else:
                C.own_rows = lambda t0, n: S.AIN[t0 // 512].ap()[t0 % 512:t0 % 512 + n, :]
                C.glob_tile = lambda gi: S.STG[gi % 8].ap()[(gi // 8) * 512:(gi // 8 + 1) * 512, :]
                C.xin_ctx = S.XC[(l - 1) % 2]
            if last_prog_layer:
                C.out_rows = lambda t0, n: O.x_out[t0:t0 + n, :]
                C.xo_ctx = O.xc_out
            else:
                C.out_rows = lambda t0, n: S.AIN[t0 // 512].ap()[t0 % 512:t0 % 512 + n, :]
                C.xo_ctx = S.XC[l % 2]
            for ph in (phase0, phase1, phase2a, phase2b, phase2c, phase3, phase4):
                A.off = base
                P.barrier()
                ph(C)
        P.emit(st)
        global _LAST_PROG
        _LAST_PROG = P
    return nc


def setup_persistent(C):
    P, A, I = C.P, C.A, C.I
    C.ident_f = A.alloc("ident_f", [128], F32)
    C.ident8 = A.alloc("ident8", [128], BF16)
    C.ones_bf = A.alloc("ones_bf", [128], BF16)
    C.ones_f = A.alloc("ones_f", [128], F32)
    C.perm = A.alloc("perm", [128], BF16)
    C.vec = A.alloc("vec", [NVEC], F32)
    C.modfm = A.alloc("modfm", [48, 2], F32)
    C.flag = A.alloc("flag", [2], F32)
    C.lam = A.alloc("lam", [4], F32)
    C.gs = A.alloc("gs", [1], F32)
    idf = C.ident_f
    P.pool(lambda e: e.memset(idf[:, :], 0.0), w=[idf.b])
    P.pool(lambda e: e.affine_select(out=idf[:, :], in_=idf[:, :], pattern=[[-1, 128]], compare_op=ALU.not_equal,
                                     fill=1.0, base=0, channel_multiplier=1), r=[idf.b], w=[idf.b])
    TS(P, 'dve', C.ident8[:, :], idf[:, :], 8.0, None, ALU.mult, None, r=[idf.b], w=[C.ident8.b])
    ob, of = C.ones_bf, C.ones_f
    P.pool(lambda e: e.memset(ob[:, :], 1.0), w=[ob.b])
    P.pool(lambda e: e.memset(of[:, :], 1.0), w=[of.b])
    P.dma('pool', C.perm[:, :], I.perm[:, :], w=[C.perm.b])
    P.dma('sp', C.flag[:, :], I.halo_flag[:, :], w=[C.flag.b])


def barrier(self):
    pend = set()
    for e in ENGS:
        if self.eng_ops[e]:
            last_c = None
            for oid in reversed(self.eng_ops[e]):
                if not self.ops[oid].dma:
                    last_c = oid
                    break
            if last_c is not None:
                pend.add(last_c)
    for q in DMAQ:
        pend.update(self.dlast[q].values())
    if getattr(self, 'last_cc', None) is not None:
        pend.add(self.last_cc)
    self.pending = {e: set(pend) for e in ENGS}


Prog.barrier = barrier
_orig_op = Prog.op


def _op(self, eng, fn, r=(), w=(), wacc=(), dma=False, cc=False):
    oid = _orig_op(self, eng, fn, r, w, wacc, dma, cc)
    pend = getattr(self, 'pending', None)
    if pend and pend.get(eng):
        self.ops[oid].deps.update(x for x in pend[eng] if x != oid)
        pend[eng] = None
    return oid


Prog.op = _op


def bcast_row(C, a, n):
    return C.I.rows[C.l, 0, a:a + n].partition_broadcast(128)


def phase0(C):
    P, A, I, S, l = C.P, C.A, C.I, C.S, C.l
    psr = Ring(C.ps)
    P.dma('sp', C.vec[:, :], I.vec[l], w=[C.vec.b])
    dl = A.alloc("dl", [256], F32)
    P.dma('sp', dl[:, :], bcast_row(C, R_LAM, 256), w=[dl.b])
    dlv = dl.ap.rearrange("p (a b c) -> p a b c", a=2, b=2)
    prod = A.alloc("prod", [2, 64], F32)
    TT(P, 'dve', prod[:, :, :], dlv[:, :, 0, :], dlv[:, :, 1, :], ALU.mult, r=[dl.b], w=[prod.b])
    s2 = A.alloc("s2", [2], F32)
    P.dve(lambda e: e.reduce_sum(out=s2[:, :], in_=prod[:, :, :], axis=AX.X), r=[prod.b], w=[s2.b])
    e2 = A.alloc("e2", [2], F32)
    ACT(P, e2[:, :], s2[:, :], AF.Exp, r=[s2.b], w=[e2.b])
    TT(P, 'dve', C.lam[:, 1:2], e2[:, 0:1], e2[:, 1:2], ALU.subtract, r=[e2.b], w=[C.lam.b])
    TS(P, 'dve', C.lam[:, 0:1], C.lam[:, 1:2], C.vec[:, V_LAM:V_LAM + 1], -1.0, ALU.add, ALU.mult,
       r=[C.lam.b, C.vec.b], w=[C.lam.b])
    TT(P, 'dve', C.gs[:, :], C.vec[:, V_DG:V_DG + 1], C.vec[:, V_LAM + 1:V_LAM + 2], ALU.mult, r=[C.vec.b], w=[C.gs.b])
    ccs = A.alloc("ccs", [8, 2], F32)
    P.dma('sp', ccs[:, :, :], I.cc[:, :, :], w=[ccs.b])
    scs = A.alloc("scs", [8, 2], F32)
    ACT(P, scs[:, :, :], ccs[:, :, :], AF.Silu, r=[ccs.b], w=[scs.b])
    scsb = A.alloc("scsb", [8, 2], BF16)
    P.dve(lambda e: e.tensor_copy(out=scsb[:, :, :], in_=scs[:, :, :]), r=[scs.b], w=[scsb.b])
    SCB = A.alloc("SCB", [2, 8, 128], BF16)
    for j in range(2):
        for k in range(8):
            ACT(P, SCB[:, j, k, :], C.ones_f[:, :], AF.Identity, r=[scs.b, C.ones_f.b], wacc=[SCB.b],
                scale=scs[:, k, j:j + 1])
    bg = A.alloc("bg", [2, 1024], F32)
    P.dma('sp', bg[:, 0, :], bcast_row(C, R_BG1, 1024), wacc=[bg.b])
    P.dma('sp', bg[:, 1, :], bcast_row(C, R_BG2, 1024), wacc=[bg.b])
    gbt = A.alloc("gbt", [4, 1024], F32)
    war = A.ring("wa", 2, [8, 512], BF16)
    for s in range(12):
        wa = war.next()
        load_w(P, wa, I.w_ada[l], 8, s * 512, (s + 1) * 512, wa.b)
        for cc in range(4):
            chunk = s * 4 + cc
            ps = psr.next()
            for k in range(8):
                MM(P, ps, ps[:, 0:2], wa[:, k, cc * 128:(cc + 1) * 128], scsb[:, k, :], k == 0, k == 7, r=[wa.b, scsb.b])
            TS(P, 'dve', C.modfm[:, chunk, :], ps[:, 0:2], C.vec[:, V_BADA + chunk:V_BADA + chunk + 1], None, ALU.add, None,
               r=[ps.b, C.vec.b], wacc=[C.modfm.b])
        if s in (4, 5, 10, 11):
            which = 0 if s < 6 else 1
            col0 = (s - (4 if s < 6 else 10)) * 512
            for j in range(2):
                ps = psr.next()
                for k in range(8):
                    MM(P, ps, ps[:, :], SCB[:, j, k, :], wa[:, k, :], k == 0, k == 7, r=[wa.b, SCB.b])
                TT(P, 'dve', gbt[:, which * 2 + j, col0:col0 + 512], ps[:, :], bg[:, which, col0:col0 + 512], ALU.add,
                   r=[ps.b, bg.b], wacc=[gbt.b])
    TS(P, 'dve', C.modfm[:, 8:16, :], C.modfm[:, 8:16, :], 1.0, None, ALU.add, None, r=[C.modfm.b], w=[C.modfm.b])
    TS(P, 'dve', C.modfm[:, 32:40, :], C.modfm[:, 32:40, :], 1.0, None, ALU.add, None, r=[C.modfm.b], w=[C.modfm.b])
    P.dma('sp', S.GB.rearrange("g p d -> p g d"), gbt[:, :, :], r=[gbt.b], w=[C.dummy])
    rp = A.alloc("rp", [120], F32, parts=32)
    oh = A.alloc("oh", [4096], F32, parts=32)
    rb = A.alloc("rb", [4096], BF16, parts=120)
    P.dma('sp', rp[:, :], I.rpbT[l], w=[rp.b])
    P.dma('sp', oh[:, :], I.oh[:, :], w=[oh.b])
    for s in range(8):
        ps = psr.next()
        MM(P, ps, ps[0:120, :], rp[:, :], oh[:, s * 512:(s + 1) * 512], True, True, r=[rp.b, oh.b])
        ACT(P, rb[:, s * 512:(s + 1) * 512], ps[0:120, :], AF.Identity, r=[ps.b], wacc=[rb.b])
    P.dma('sp', S.RPBF.rearrange("h d k q -> (h d) (k q)"), rb[:, :], r=[rb.b], w=[C.dummy])
    if l == 0:
        zt = A.alloc("zt", [3, 4, CTXL + 32], BF16)
        P.pool(lambda e: e.memset(zt[:, :, :, :], 0.0), w=[zt.b])
        P.dma('sp', S.CONVC.rearrange("a p c t -> p a c t"), zt[:, :, :, :], r=[zt.b], w=[C.dummy])


def phase1(C):
    P, A, I, S, l = C.P, C.A, C.I, C.S, C.l
    psr = Ring(C.ps)
    W = A.alloc("w1in", [8, 5632], BF16)
    load_w(P, W, I.w_in[l], 8, 0, 5632, W.b)
    bdv = A.alloc("bdv", [2, 512], F32)
    P.dma('sp', bdv[:, 0, :], bcast_row(C, R_BDV, 512), wacc=[bdv.b])
    P.dma('sp', bdv[:, 1, :], bcast_row(C, R_BNV, 512), wacc=[bdv.b])
    xt = A.alloc("xt", [4, 1024], F32)
    uTr = A.ring("uT", 2, [8, 512], BF16)
    ropr = A.ring("rope", 2, [2, 512], F32)
    tf = A.ring("tf", 4, [512], F32)
    tb = A.ring("tb", 6, [512], BF16)
    qbr = A.ring("qb", 2, [512], BF16)
    vec, dummy = C.vec, C.dummy

    def bias(chunk):
        return vec[:, V_BIN + chunk:V_BIN + chunk + 1]

    def tile(kind, ti):
        N = 256 if kind == 'ctx' else 512
        nsub = N // 128
        j = 1 if kind == 'ctx' else 0
        t0 = 0 if kind == 'ctx' else ti * 512
        if kind == 'own':
            src = C.own_rows(t0, N)
            rdeps = [C.ain_b[ti]]
        elif kind == 'glob':
            src = C.glob_tile(ti)
            rdeps = [C.stg_b[ti % 8]]
        else:
            src = C.xin_ctx[0:N, :]
            rdeps = []
        P.dma('sp', xt[:, 0:nsub, :], src.rearrange("(j p) f -> p j f", p=128), r=rdeps, w=[xt.b])
        uT = uTr.next()
        for k in range(8):
            ps = psr.next()
            for jj in range(nsub):
                TR(P, ps, ps[:, jj * 128:(jj + 1) * 128], xt[:, jj, k * 128:(k + 1) * 128], C.ident_f[:, :],
                   r=[xt.b, C.ident_f.b])
            ACT(P, uT[:, k, 0:N], ps[:, 0:N], AF.Identity, r=[ps.b, C.modfm.b], wacc=[uT.b],
                scale=C.modfm[:, 8 + k, j:j + 1], bias=C.modfm[:, k, j:j + 1])
        rope = kind != 'ctx'
        if kind != 'glob':
            tok0 = t0 if kind == 'own' else SOWN
            P.dma('sp', S.UT[:, :, tok0:tok0 + N], uT[:, :, 0:N], r=[uT.b], w=[dummy])
        if rope:
            rp = ropr.next()
            rsrc = I.rope_own if kind == 'own' else I.rope_seq
            P.dma('sp', rp[:, :, :], rsrc[:, :, t0:t0 + 512], w=[rp.b])

        def fm(chunk, n0=0, n1=N):
            ps = psr.next()
            for k in range(8):
                MM(P, ps, ps[:, 0:n1 - n0], W[:, k, chunk * 128:(chunk + 1) * 128], uT[:, k, n0:n1], k == 0, k == 7,
                   r=[W.b, uT.b])
            return ps

        def plain(chunk, dst_ap, n0=0, n1=N, func=AF.Identity):
            ps = fm(chunk, n0, n1)
            o = tb.next()
            n = n1 - n0
            ACT(P, o[:, 0:n], ps[:, 0:n], func, r=[ps.b, vec.b], w=[o.b], bias=bias(chunk))
            P.dma('sp', dst_ap, o[:, 0:n], r=[o.b], w=[dummy])

        def prod2(chunk_a, func_a, chunk_b, dst_ap, n0=0, n1=N, flag=None):
            n = n1 - n0
            ps1 = fm(chunk_a, n0, n1)
            a = tf.next()
            ACT(P, a[:, 0:n], ps1[:, 0:n], func_a, r=[ps1.b, vec.b], w=[a.b], bias=bias(chunk_a))
            ps2 = fm(chunk_b, n0, n1)
            o = tb.next()
            if flag is None:
                STT(P, 'dve', o[:, 0:n], ps2[:, 0:n], bias(chunk_b), a[:, 0:n], ALU.add, ALU.mult,
                    r=[ps2.b, a.b, vec.b], w=[o.b])
            else:
                a2 = tf.next()
                STT(P, 'dve', a2[:, 0:n], ps2[:, 0:n], bias(chunk_b), a[:, 0:n], ALU.add, ALU.mult,
                    r=[ps2.b, a.b, vec.b], w=[a2.b])
                TS(P, 'dve', o[:, 0:n], a2[:, 0:n], C.flag[:, flag:flag + 1], None, ALU.mult, None,
                   r=[a2.b, C.flag.b], w=[o.b])
            P.dma('sp', dst_ap, o[:, 0:n], r=[o.b], w=[dummy])

        def roped(chunk, dst_ap):
            ps = fm(chunk)
            if not rope:
                o = tb.next()
                ACT(P, o[:, 0:N], ps[:, 0:N], AF.Identity, r=[ps.b, vec.b], w=[o.b], bias=bias(chunk))
            else:
                qb = qbr.next()
                ACT(P, qb[:, 0:N], ps[:, 0:N], AF.Identity, r=[ps.b, vec.b], w=[qb.b], bias=bias(chunk))
                ps2 = psr.next()
                MM(P, ps2, ps2[:, 0:N], C.perm[:, :], qb[:, 0:N], True, True, r=[C.perm.b, qb.b])
                t1 = tf.next()
                TT(P, 'dve', t1[:, 0:N], qb[:, 0:N], rp[:, 0, 0:N], ALU.mult, r=[qb.b, rp.b], w=[t1.b])
                t2 = tf.next()
                TT(P, 'dve', t2[:, 0:N], ps2[:, 0:N], rp[:, 1, 0:N], ALU.mult, r=[ps2.b, rp.b], w=[t2.b])
                o = tb.next()
                TT(P, 'dve', o[:, 0:N], t1[:, 0:N], t2[:, 0:N], ALU.add, r=[t1.b, t2.b], w=[o.b])
            P.dma('sp', dst_ap, o[:, 0:N], r=[o.b], w=[dummy])

        def tm(col0, bi, dst, rowoff, subs):
            for jj in subs:
                ps = psr.next()
                for k in range(8):
                    MM(P, ps, ps[:, :], uT[:, k, jj * 128:(jj + 1) * 128], W[:, k, col0:col0 + 512], k == 0, k == 7,
                       r=[W.b, uT.b])
                o = tb.next()
                TT(P, 'dve', o[:, :], ps[:, :], bdv[:, bi, :], ALU.add, r=[ps.b, bdv.b], w=[o.b])
                P.dma('sp', dst[rowoff + jj * 128:rowoff + (jj + 1) * 128, :], o[:, :], r=[o.b], w=[dummy])

        if kind in ('own', 'ctx'):
            cv = S.CONV if kind == 'own' else S.CONVC
            co = 16 + t0
            qo = t0 if kind == 'own' else SOWN
            ko = 256 + t0 if kind == 'own' else 0
            for c in range(4):
                plain(c, cv[0, :, c, co:co + N])
                prod2(4 + c, AF.Identity, 8 + c, cv[1, :, c, co:co + N])
                prod2(16 + c, AF.Sigmoid, 12 + c, cv[2, :, c, co:co + N])
            for c in range(4):
                roped(20 + c, S.QD[:, c, qo:qo + N])
                if kind == 'ctx':
                    roped(24 + c, S.KD[:, c, 0:N])
            for c in range(4):
                plain(32 + c, S.QN[:, c, qo:qo + N])
                if kind == 'own':
                    plain(36 + c, S.KN[:, c, ko:ko + N])
                else:
                    plain(36 + c, S.KNC[:, c, 0:N])
            if kind == 'own':
                tm(5120, 1, S.VN, ko, range(nsub))
            else:
                tm(3584, 0, S.VD, 0, range(nsub))
                tm(5120, 1, S.VNC, 0, range(nsub))
        else:
            ko = 256 + t0
            for c in range(4):
                roped(24 + c, S.KD[:, c, ko:ko + N])
            tm(3584, 0, S.VD, ko, range(nsub))
            if ti == 8:
                for c in range(4):
                    plain(36 + c, S.KN[:, c, 4352:4608], 0, 256)
                    prod2(4 + c, AF.Identity, 8 + c, S.CONV[1, :, c, 4112:4128], 0, 16, flag=1)
                    prod2(16 + c, AF.Sigmoid, 12 + c, S.CONV[2, :, c, 4112:4128], 0, 16, flag=1)
                tm(5120, 1, S.VN, 4352, range(0, 2))
            if ti == 7:
                for c in range(4):
                    plain(36 + c, S.KN[:, c, 0:256], 256, 512)
                    prod2(4 + c, AF.Identity, 8 + c, S.CONV[1, :, c, 0:16], 496, 512, flag=0)
                    prod2(16 + c, AF.Sigmoid, 12 + c, S.CONV[2, :, c, 0:16], 496, 512, flag=0)
                tm(5120, 1, S.VN, -256, range(2, 4))

    for ti in range(8):
        tile('own', ti)
    tile('ctx', 0)
    for ti in range(16):
        tile('glob', ti)


def phase2a(C):
    P, A, I, S, l = C.P, C.A, C.I, C.S, C.l
    psr = Ring(C.ps)
    vec, dummy = C.vec, C.dummy
    D3 = A.alloc("D3", [12, 128], BF16)
    D31 = A.alloc("D31", [124, 128], BF16)
    for i in range(12):
        TS(P, 'pool' if i % 2 else 'dve', D3[:, i, :], C.ident_f[:, :], vec[:, V_CW + i:V_CW + i + 1], None, ALU.mult, None,
           r=[C.ident_f.b, vec.b], wacc=[D3.b])
    for i in range(124):
        TS(P, 'pool' if i % 2 else 'dve', D31[:, i, :], C.ident_f[:, :], vec[:, V_DW + i:V_DW + i + 1], None, ALU.mult, None,
           r=[C.ident_f.b, vec.b], wacc=[D31.b])
    abr = A.ring("ab", 2, [4, 512], BF16)
    cxr = A.ring("cx", 2, [4, 514], BF16)
    glr = A.ring("gl", 2, [4, 542], BF16)
    z = A.alloc("z", [4, 512], F32)
    zsq = A.alloc("zsq", [4, 512], F32)
    mean = A.alloc("mean", [512], F32)
    msq = A.alloc("msq", [512], F32)
    var = A.alloc("var", [512], F32)
    rstd = A.alloc("rstd", [512], F32)
    tf = A.ring("tf", 3, [512], F32)
    tb = A.ring("tb", 4, [512], BF16)
    tiles = [('own', ti) for ti in range(8)] + [('ctx', 0)]
    for kind, ti in tiles:
        N = 512 if kind == 'own' else 256
        cv = S.CONV if kind == 'own' else S.CONVC
        co = 16 + ti * 512
        yo = ti * 512 if kind == 'own' else SOWN
        ab, cx, gl = abr.next(), cxr.next(), glr.next()
        P.dma('sp', ab[:, :, 0:N], cv[0, :, :, co:co + N], w=[ab.b])
        P.dma('sp', cx[:, :, 0:N + 2], cv[1, :, :, co - 1:co + N + 1], w=[cx.b])
        P.dma('sp', gl[:, :, 0:N + 30], cv[2, :, :, co - 15:co + N + 15], w=[gl.b])
        for c in range(4):
            ps = psr.next()
            for k in range(3):
                MM(P, ps, ps[:, 0:N], D3[:, k * 4 + c, :], cx[:, c, k:k + N], k == 0, k == 2, r=[D3.b, cx.b])
            o = tb.next()
            STT(P, 'dve', o[:, 0:N], ps[:, 0:N], vec[:, V_CB + c:V_CB + c + 1], ab[:, c, 0:N], ALU.add, ALU.mult,
                r=[ps.b, ab.b, vec.b], w=[o.b])
            P.dma('sp', S.Y[:, c, yo:yo + N], o[:, 0:N], r=[o.b], w=[dummy])
        for c in range(4):
            ps = psr.next()
            for k in range(31):
                MM(P, ps, ps[:, 0:N], D31[:, k * 4 + c, :], gl[:, c, k:k + N], k == 0, k == 30, r=[D31.b, gl.b])
            ACT(P, z[:, c, 0:N], ps[:, 0:N], AF.Identity, r=[ps.b, vec.b], wacc=[z.b], bias=vec[:, V_DB + c:V_DB + c + 1])
            ACT(P, zsq[:, c, 0:N], ps[:, 0:N], AF.Square, r=[ps.b, vec.b], wacc=[zsq.b], bias=vec[:, V_DB + c:V_DB + c + 1])
        psm = psr.next()
        for c in range(4):
            MM(P, psm, psm[:, 0:N], C.ones_f[:, :], z[:, c, 0:N], c == 0, c == 3, r=[C.ones_f.b, z.b])
        pss = psr.next()
        for c in range(4):
            MM(P, pss, pss[:, 0:N], C.ones_f[:, :], zsq[:, c, 0:N], c == 0, c == 3, r=[C.ones_f.b, zsq.b])
        ACT(P, mean[:, 0:N], psm[:, 0:N], AF.Identity, r=[psm.b], w=[mean.b], scale=1.0 / 512)
        TT(P, 'dve', msq[:, 0:N], mean[:, 0:N], mean[:, 0:N], ALU.mult, r=[mean.b], w=[msq.b])
        STT(P, 'dve', var[:, 0:N], pss[:, 0:N], 1.0 / 512, msq[:, 0:N], ALU.mult, ALU.subtract, r=[pss.b, msq.b], w=[var.b])
        TS(P, 'dve', var[:, 0:N], var[:, 0:N], LN_EPS, None, ALU.add, None, r=[var.b], w=[var.b])
        ACT(P, var[:, 0:N], var[:, 0:N], AF.Sqrt, r=[var.b], w=[var.b])
        P.dve(lambda e, N=N: e.reciprocal(out=rstd[:, 0:N], in_=var[:, 0:N]), r=[var.b], w=[rstd.b])
        for c in range(4):
            t = tf.next()
            TT(P, 'dve', t[:, 0:N], z[:, c, 0:N], mean[:, 0:N], ALU.subtract, r=[z.b, mean.b], w=[t.b])
            TT(P, 'dve', t[:, 0:N], t[:, 0:N], rstd[:, 0:N], ALU.mult, r=[t.b, rstd.b], w=[t.b])
            o = tb.next()
            ACT(P, o[:, 0:N], t[:, 0:N], AF.Silu, r=[t.b, vec.b], w=[o.b], scale=vec[:, V_LG + c:V_LG + c + 1],
                bias=vec[:, V_LB + c:V_LB + c + 1])
            P.dma('sp', S.Y[:, 4 + c, yo:yo + N], o[:, 0:N], r=[o.b], w=[dummy])


def phase2b(C):
    P, A, I, S, l = C.P, C.A, C.I, C.S, C.l
    dummy = C.dummy
    Kr = A.ring("Kh", 2, [8448], BF16)
    Vr = A.ring("Vh", 2, [66, 128], BF16)
    Qr = A.ring("Qh", 2, [NTOK], BF16)
    ptr = A.ring("pt", 6, [2, 512], BF16)
    accr = A.ring("acc", 4, [2, 512], F32)
    Ocr = A.ring("Oc", 4, [512], F32)
    om = [A.alloc(f"om{m}", [512], F32) for m in range(2)]
    lnr = A.ring("lnr", 2, [512], F32)
    rs = A.ring("rs", 2, [512], F32)
    o = A.alloc("o", [512], F32)
    osq = A.alloc("osq", [512], F32)
    rr = A.alloc("rr", [512], F32)
    tb = A.ring("tb", 2, [512], BF16)
    pss = Ring(C.ps[0:6])
    pso = [C.ps[6], C.ps[7]]
    VDv = S.VD.rearrange("(c p) v -> p c v", p=128)
    LA = 2
    pend = []

    def tail(N, yo, h, acc2, Oc):
        acc = acc2[0]
        TT(P, 'dve', acc[:, :, 0:N], acc[:, :, 0:N], acc2[1][:, :, 0:N], ALU.add, r=[acc.b, acc2[1].b], w=[acc.b])
        for m in range(2):
            psu = pss.next()
            MM(P, psu, psu[:, 0:N], C.ones_f[:, :], acc[:, m, 0:N], True, True, r=[C.ones_f.b, acc.b])
            lt = lnr.next()
            ACT(P, lt[:, 0:N], psu[:, 0:N], AF.Ln, r=[psu.b], w=[lt.b])
            r1 = rs.next()
            ACT(P, r1[:, 0:N], lt[:, 0:N], AF.Exp, r=[lt.b], w=[r1.b], scale=-1.0)
            TT(P, 'dve', om[m][:, 0:N], Oc[m][:, 0:N], r1[:, 0:N], ALU.mult, r=[Oc[m].b, r1.b], w=[om[m].b])
        STT(P, 'dve', o[:, 0:N], om[1][:, 0:N], C.lam[:, 0:1], om[0][:, 0:N], ALU.mult, ALU.add,
            r=[om[0].b, om[1].b, C.lam.b], w=[o.b])
        TT(P, 'dve', osq[:, 0:N], o[:, 0:N], o[:, 0:N], ALU.mult, r=[o.b], w=[osq.b])
        psn = pss.next()
        MM(P, psn, psn[:, 0:N], C.ones_f[:, :], osq[:, 0:N], True, True, r=[C.ones_f.b, osq.b])
        TS(P, 'dve', rr[:, 0:N], psn[:, 0:N], 1.0 / 128, LN_EPS, ALU.mult, ALU.add, r=[psn.b], w=[rr.b])
        ACT(P, rr[:, 0:N], rr[:, 0:N], AF.Ln, r=[rr.b], w=[rr.b])
        ACT(P, rr[:, 0:N], rr[:, 0:N], AF.Exp, r=[rr.b], w=[rr.b], scale=-0.5)
        ob = tb.next()
        STT(P, 'dve', ob[:, 0:N], o[:, 0:N], C.gs[:, 0:1], rr[:, 0:N], ALU.mult, ALU.mult, r=[o.b, rr.b, C.gs.b],
            w=[ob.b])
        P.dma('sp', S.Y[:, 8 + h, yo:yo + N], ob[:, 0:N], r=[ob.b], w=[dummy])

    def flush():
        while pend:
            tail(*pend.pop(0))

    def copy_out(dst, src, N):
        P.op('dve', lambda e: e.tensor_copy(out=dst[:, 0:N], in_=src[:, 0:N]), r=[src.b], w=[dst.b])

    def acc_first(acc, pt, N):
        P.op('dve', lambda e: e.tensor_copy(out=acc[:, :, 0:N], in_=pt[:, :, 0:N]), r=[pt.b], w=[acc.b])

    def qtile(h, qt, Kh, Vh, Qh):
        N = 512 if qt < 8 else 256
        q0 = qt * 512
        chunks = list(range(66)) if qt < 8 else [0, 1]
        n = len(chunks)
        sb = {}
        acc2 = [accr.next(), accr.next()]

        def qk(ci):
            c = chunks[ci]
            banks = []
            for m in range(2):
                lo, hi = m * 64, (m + 1) * 64
                ps = pss.next()
                MM(P, ps, ps[:, 0:N], Kh[lo:hi, c * 128:(c + 1) * 128], Qh[lo:hi, q0:q0 + N], True, True,
                   r=[Kh.b, Qh.b])
                banks.append(ps)
            sb[ci] = banks

        for ci in range(min(LA, n)):
            qk(ci)
        for ci in range(n):
            if ci + LA < n:
                qk(ci + LA)
            c = chunks[ci]
            pt = ptr.next()
            for m in range(2):
                ps = sb[ci][m]
                ACT(P, pt[:, m, 0:N], ps[:, 0:N], AF.Exp, r=[ps.b], wacc=[pt.b], scale=0.125)
            del sb[ci]
            for m in range(2):
                MM(P, pso[m], pso[m][:, 0:N], Vh[:, c, :], pt[:, m, 0:N], ci == 0, ci == n - 1, r=[Vh.b, pt.b])
            acc = acc2[ci % 2]
            if ci < 2:
                acc_first(acc, pt, N)
            else:
                TT(P, 'dve', acc[:, :, 0:N], acc[:, :, 0:N], pt[:, :, 0:N], ALU.add, r=[acc.b, pt.b], w=[acc.b])
            if ci == 6:
                flush()
        flush()
        Oc = [Ocr.next(), Ocr.next()]
        for m in range(2):
            copy_out(Oc[m], pso[m], N)
        yo = q0 if qt < 8 else SOWN
        pend.append((N, yo, h, acc2, Oc))

    for h in range(4):
        Kh, Vh, Qh = Kr.next(), Vr.next(), Qr.next()
        P.dma('sp', Kh[:, :], S.KD[:, h, :], w=[Kh.b])
        P.dma('sp', Qh[:, :], S.QD[:, h, :], w=[Qh.b])
        for a in range(0, 66, 22):
            P.dma('sp', Vh[:, a:a + 22, :], VDv[:, a:a + 22, h * 128:(h + 1) * 128], wacc=[Vh.b])
        for qt in range(9):
            qtile(h, qt, Kh, Vh, Qh)
    flush()


def phase2c(C):
    P, A, I, S, l = C.P, C.A, C.I, C.S, C.l
    dummy = C.dummy
    a16 = A.alloc("a16", [8, 128], BF16, parts=16)
    rm = A.alloc("rm", [8, 512], BF16, parts=16)
    P.dma('pool', a16[:, :, :], I.a16.rearrange("r (c p) -> r c p", c=8), w=[a16.b])
    for b in range(8):
        P.dma('pool', rm[:, b, :], I.rm[:, b * 512:(b + 1) * 512], wacc=[rm.b])
    Kr = A.ring("Kc", 2, [SOWN + 512], BF16)
    Kcr = A.ring("Kcc", 2, [CTXL], BF16)
    Qr = A.ring("Qc", 2, [NTOK], BF16)
    Vr = A.ring("Vh", 2, [36, 64], BF16)
    Vcr = A.ring("Vch", 2, [2, 64], BF16)
    Br = A.ring("bias", 2, [8, 512], BF16)
    ptr = A.ring("pt", 6, [512], BF16)
    rs = A.ring("rs", 2, [512], F32, parts=64)
    tb = A.ring("tb", 3, [512], BF16, parts=64)
    pss = Ring(C.ps[0:4])
    accs = Ring([(C.ps[4], C.ps[5]), (C.ps[6], C.ps[7])])
    VNv = S.VN.rearrange("(c p) v -> p c v", p=128)
    VNCv = S.VNC.rearrange("(c p) v -> p c v", p=128)
    Kc = Kcc = Qc = None
    for h in range(8):
        hc, po = h // 2, (h % 2) * 64
        if h % 2 == 0:
            Kc, Kcc, Qc = Kr.next(), Kcr.next(), Qr.next()
            P.dma('sp', Kc[:, :], S.KN[:, hc, :], w=[Kc.b])
            P.dma('sp', Kcc[:, :], S.KNC[:, hc, :], w=[Kcc.b])
            P.dma('sp', Qc[:, :], S.QN[:, hc, :], w=[Qc.b])
        Vh, Vch, Bh = Vr.next(), Vcr.next(), Br.next()
        for a in range(0, 36, 12):
            P.dma('sp', Vh[:, a:a + 12, :], VNv[:, a:a + 12, h * 64:(h + 1) * 64], wacc=[Vh.b])
        P.dma('sp', Vch[:, :, :], VNCv[:, :, h * 64:(h + 1) * 64], w=[Vch.b])
        P.pool(lambda e, Bh=Bh: e.memset(Bh[:, :, :], 0.0), w=[Bh.b])
        for rq in range(8):
            for par in range(2):
                clo = max(0, -((-(rq - 3 - par)) // 2))
                chi = min(7, (rq + 11 - par) // 2)
                n = chi - clo + 1
                d0 = 2 * clo + par - rq + 3
                src = S.RPBF[h, d0:d0 + 2 * n - 1:2, :, :].rearrange("i k q -> k i q")
                P.dma('sp', Bh[par * 64:(par + 1) * 64, clo:chi + 1, rq * 64:(rq + 1) * 64], src, wacc=[Bh.b])
        for blk in range(9):
            N = 512 if blk < 8 else 256
            q0 = blk * 512
            pso, psu = accs.next()
            nch = 10 if blk < 8 else 2
            sb = {}

            def qk(ci):
                ps = pss.next()
                if blk < 8 and ci < 8:
                    k0 = blk * 512 + ci * 128
                    MM(P, ps, ps[:, 0:N], Kc[po:po + 64, k0:k0 + 128], Qc[po:po + 64, q0:q0 + N], True, False,
                       r=[Kc.b, Qc.b])
                    MM(P, ps, ps[:, 0:N], C.ident8[:, :], Bh[:, ci, :], False, False, r=[C.ident8.b, Bh.b])
                    MM(P, ps, ps[:, 0:N], a16[:, ci, :], rm[:, blk, :], False, True, r=[a16.b, rm.b])
                else:
                    cc = ci - 8 if blk < 8 else ci
                    MM(P, ps, ps[:, 0:N], Kcc[po:po + 64, cc * 128:(cc + 1) * 128], Qc[po:po + 64, q0:q0 + N], True, True,
                       r=[Kcc.b, Qc.b])
                sb[ci] = ps

            LA = 2
            for ci in range(min(LA, nch)):
                qk(ci)
            for ci in range(nch):
                if ci + LA < nch:
                    qk(ci + LA)
                ps = sb.pop(ci)
                if blk < 8 and ci < 8:
                    vl = Vh[:, blk * 4 + ci, :]
                    vb = Vh.b
                else:
                    cc = ci - 8 if blk < 8 else ci
                    vl = Vch[:, cc, :]
                    vb = Vch.b
                pt = ptr.next()
                ACT(P, pt[:, 0:N], ps[:, 0:N], AF.Exp, r=[ps.b], w=[pt.b], scale=0.125)
                MM(P, pso, pso[0:64, 0:N], vl, pt[:, 0:N], ci == 0, ci == nch - 1, r=[vb, pt.b])
                MM(P, psu, psu[0:64, 0:N], C.ones_bf[:, 0:64], pt[:, 0:N], ci == 0, ci == nch - 1, r=[C.ones_bf.b, pt.b])
            r1 = rs.next()
            P.dve(lambda e, r1=r1, psu=psu, N=N: e.reciprocal(out=r1[:, 0:N], in_=psu[0:64, 0:N]), r=[psu.b], w=[r1.b])
            ob = tb.next()
            TT(P, 'dve', ob[:, 0:N], pso[0:64, 0:N], r1[:, 0:N], ALU.mult, r=[pso.b, r1.b], w=[ob.b])
            yo = q0 if blk < 8 else SOWN
            P.dma('sp', S.Y[po:po + 64, 12 + hc, yo:yo + N], ob[:, 0:N], r=[ob.b], w=[dummy])


def resid_ln(C, P, pbs, xres, gb, lng, lnb, dst_rows, tmp_r, tmp_t, st6, mv, rsd, final, wbuf=None):
    t, r = tmp_t, tmp_r
    for hf in range(2):
        TT(P, 'dve', t[:, hf * 512:(hf + 1) * 512], pbs[hf][:, :], gb[:, hf * 512:(hf + 1) * 512], ALU.mult,
           r=[pbs[hf].b, gb.b], wacc=[t.b])
    STT(P, 'dve', r[:, :], xres, ALPHA, t[:, :], ALU.mult, ALU.add, r=[t.b] + xres_b(C), w=[r.b])
    for hf in range(2):
        P.dve(lambda e, hf=hf: e.bn_stats(out=st6[:, hf, :], in_=r[:, hf * 512:(hf + 1) * 512]), r=[r.b], wacc=[st6.b])
    P.dve(lambda e: e.bn_aggr(out=mv[:, :], in_=st6[:, :, :]), r=[st6.b], w=[mv.b])
    TS(P, 'dve', rsd[:, :], mv[:, 1:2], LN_EPS, None, ALU.add, None, r=[mv.b], w=[rsd.b])
    ACT(P, rsd[:, :], rsd[:, :], AF.Sqrt, r=[rsd.b], w=[rsd.b])
    P.dve(lambda e: e.reciprocal(out=rsd[:, :], in_=rsd[:, :]), r=[rsd.b], w=[rsd.b])
    TS(P, 'dve', r[:, :], r[:, :], mv[:, 0:1], rsd[:, 0:1], ALU.subtract, ALU.mult, r=[r.b, mv.b, rsd.b], w=[r.b])
    TT(P, 'dve', r[:, :], r[:, :], lng[:, :], ALU.mult, r=[r.b, lng.b], w=[r.b])
    TT(P, 'dve', r[:, :], r[:, :], lnb[:, :], ALU.add, r=[r.b, lnb.b], w=[r.b])
    if wbuf is None:
        P.dma('sp', dst_rows, r[:, :], r=[r.b], w=[C.dummy], final=final)
    else:
        P.dma('sp', dst_rows, r[:, :], r=[r.b], wacc=[wbuf], final=final)


def xres_b(C):
    return [C._xres_b]


def phase3(C):
    P, A, I, S, l = C.P, C.A, C.I, C.S, C.l
    vec = C.vec
    Wg = A.alloc("Wg", [8, 4096], BF16)
    Wb = A.alloc("Wb", [16, 1024], BF16)
    Wo = A.alloc("Wo", [8, 1024], BF16)
    load_w(P, Wg, I.w_in[l], 8, 5632, 9728, Wg.b)
    load_w(P, Wb, I.w_branch[l], 16, 0, 1024, Wb.b)
    load_w(P, Wo, I.w_o[l], 8, 0, 1024, Wo.b)
    gb = A.alloc("gb", [2, 1024], F32)
    P.dma('sp', gb[:, :, :], S.GB[0:2].rearrange("g p d -> p g d"), w=[gb.b])
    lng = A.alloc("lng", [1024], F32)
    lnb = A.alloc("lnb", [1024], F32)
    P.dma('sp', lng[:, :], bcast_row(C, R_LNG0, 1024), w=[lng.b])
    P.dma('sp', lnb[:, :], bcast_row(C, R_LNB0, 1024), w=[lnb.b])
    uTr = A.ring("uT", 2, [8, 256], BF16)
    Yr = A.ring("Yt", 2, [16, 256], BF16)
    xtr = A.ring("xt", 2, [2, 1024], F32)
    mT = A.alloc("mT", [8, 256], BF16)
    sg = A.ring("sg", 3, [256], F32)
    tm_ = A.ring("tm", 3, [256], F32)
    accr = A.ring("acc", 2, [256], F32)
    tr_ = A.ring("tr", 2, [1024], F32)
    tt_ = A.ring("tt", 2, [1024], F32)
    st6 = A.alloc("st6", [2, 6], F32)
    mv = A.alloc("mv", [2], F32)
    rsd = A.alloc("rsd", [1], F32)
    psr = Ring(C.ps[0:4])
    pso = Ring([(C.ps[4], C.ps[5]), (C.ps[6], C.ps[7])])
    N = 256
    for ti in range(NTOK // N):
        t0 = ti * N
        isctx = t0 >= SOWN
        j = 1 if isctx else 0
        uT, Yt, xt = uTr.next(), Yr.next(), xtr.next()
        P.dma('sp', uT[:, :, :], S.UT[:, :, t0:t0 + N], w=[uT.b])
        P.dma('sp', Yt[:, :, :], S.Y[:, :, t0:t0 + N], w=[Yt.b])
        xsrc = C.xin_ctx[t0 - SOWN:t0 - SOWN + N, :] if isctx else C.own_rows(t0, N)
        P.dma('sp', xt[:, :, :], xsrc.rearrange("(j p) f -> p j f", p=128), r=([] if isctx else [C.ain_b[t0 // 512]]), w=[xt.b])
        for oc in range(8):
            acc = accr.next()
            for i in range(4):
                ps = psr.next()
                gcol = i * 1024 + oc * 128
                for k in range(8):
                    MM(P, ps, ps[:, 0:N], Wg[:, k, gcol:gcol + 128], uT[:, k, :], k == 0, k == 7, r=[Wg.b, uT.b])
                g = sg.next()
                gch = 44 + i * 8 + oc
                ACT(P, g[:, :], ps[:, 0:N], AF.Sigmoid, r=[ps.b, vec.b], w=[g.b], bias=vec[:, V_BIN + gch:V_BIN + gch + 1])
                ps2 = psr.next()
                for k in range(4):
                    MM(P, ps2, ps2[:, 0:N], Wb[:, i * 4 + k, oc * 128:(oc + 1) * 128], Yt[:, i * 4 + k, :], k == 0, k == 3,
                       r=[Wb.b, Yt.b])
                bbc = V_BB + i * 8 + oc
                if i == 0:
                    STT(P, 'dve', acc[:, :], ps2[:, 0:N], vec[:, bbc:bbc + 1], g[:, :], ALU.add, ALU.mult,
                        r=[ps2.b, g.b, vec.b], w=[acc.b])
                else:
                    tmv = tm_.next()
                    STT(P, 'dve', tmv[:, :], ps2[:, 0:N], vec[:, bbc:bbc + 1], g[:, :], ALU.add, ALU.mult,
                        r=[ps2.b, g.b, vec.b], w=[tmv.b])
                    if i < 3:
                        TT(P, 'dve', acc[:, :], acc[:, :], tmv[:, :], ALU.add, r=[acc.b, tmv.b], w=[acc.b])
                    else:
                        TT(P, 'dve', mT[:, oc, :], acc[:, :], tmv[:, :], ALU.add, r=[acc.b, tmv.b], wacc=[mT.b])
        for jj in range(2):
            pb = pso.next()
            for hf in range(2):
                for k in range(8):
                    MM(P, pb[hf], pb[hf][:, :], mT[:, k, jj * 128:(jj + 1) * 128], Wo[:, k, hf * 512:(hf + 1) * 512],
                       k == 0, k == 7, r=[mT.b, Wo.b])
            C._xres_b = xt.b
            resid_ln(C, P, pb, xt[:, jj, :], T(gb[:, j, :], gb.b), lng, lnb,
                     S.X1[t0 + jj * 128:t0 + (jj + 1) * 128, :], tr_.next(), tt_.next(), st6, mv, rsd, False)


def phase4(C):
    P, A, I, S, l = C.P, C.A, C.I, C.S, C.l
    W1 = A.alloc("W1", [8, 4096], BF16)
    W2 = A.alloc("W2", [32, 1024], BF16)
    load_w(P, W1, I.w_ff1[l], 8, 0, 4096, W1.b)
    load_w(P, W2, I.w_ff2[l], 32, 0, 1024, W2.b)
    gb = A.alloc("gb", [2, 1024], F32)
    P.dma('sp', gb[:, :, :], S.GB[2:4].rearrange("g p d -> p g d"), w=[gb.b])
    lng = A.alloc("lng", [1024], F32)
    lnb = A.alloc("lnb", [1024], F32)
    P.dma('sp', lng[:, :], bcast_row(C, R_LNG1, 1024), w=[lng.b])
    P.dma('sp', lnb[:, :], bcast_row(C, R_LNB1, 1024), w=[lnb.b])
    xtr = A.ring("xt", 2, [2, 1024], F32)
    uT = A.alloc("u2T", [8, 256], BF16)
    hT = A.alloc("hT", [32, 256], BF16)
    sq = A.ring("sq", 3, [256], F32)
    tr_ = A.ring("tr", 2, [1024], F32)
    tt_ = A.alloc("tt", [1024], F32)
    st6 = A.alloc("st6", [2, 6], F32)
    mv = A.alloc("mv", [2], F32)
    rsd = A.alloc("rsd", [1], F32)
    psr = Ring(C.ps[0:4])
    pso = Ring([(C.ps[4], C.ps[5]), (C.ps[6], C.ps[7])])
    N = 256
    for ti in range(NTOK // N):
        t0 = ti * N
        isctx = t0 >= SOWN
        j = 1 if isctx else 0
        xt = xtr.next()
        P.dma('sp', xt[:, :, :], S.X1[t0:t0 + N, :].rearrange("(j p) f -> p j f", p=128), w=[xt.b])
        for k in range(8):
            ps = psr.next()
            for jj in range(2):
                TR(P, ps, ps[:, jj * 128:(jj + 1) * 128], xt[:, jj, k * 128:(k + 1) * 128], C.ident_f[:, :],
                   r=[xt.b, C.ident_f.b])
            ACT(P, uT[:, k, :], ps[:, 0:N], AF.Identity, r=[ps.b, C.modfm.b], wacc=[uT.b],
                scale=C.modfm[:, 32 + k, j:j + 1], bias=C.modfm[:, 24 + k, j:j + 1])
        for fc in range(32):
            ps = psr.next()
            for k in range(8):
                MM(P, ps, ps[:, 0:N], W1[:, k, fc * 128:(fc + 1) * 128], uT[:, k, :], k == 0, k == 7, r=[W1.b, uT.b])
            s = sq.next()
            ACT(P, s[:, :], ps[:, 0:N], AF.Square, r=[ps.b], w=[s.b])
            STT(P, 'dve', hT[:, fc, :], ps[:, 0:N], 0.0, s[:, :], ALU.is_gt, ALU.mult, r=[ps.b, s.b], wacc=[hT.b])
        for jj in range(2):
            pb = pso.next()
            for hf in range(2):
                for fc in range(32):
                    MM(P, pb[hf], pb[hf][:, :], hT[:, fc, jj * 128:(jj + 1) * 128], W2[:, fc, hf * 512:(hf + 1) * 512],
                       fc == 0, fc == 31, r=[hT.b, W2.b])
            C._xres_b = xt.b
            if isctx:
                dst = C.xo_ctx[t0 - SOWN + jj * 128:t0 - SOWN + (jj + 1) * 128, :]
            else:
                dst = C.out_rows(t0 + jj * 128, 128)
            resid_ln(C, P, pb, xt[:, jj, :], T(gb[:, j, :], gb.b), lng, lnb, dst, tr_.next(), tt_, st6, mv, rsd, C.final,
                     wbuf=(None if (isctx or C.final) else C.ain_b[t0 // 512]))
        if (not C.final) and (not isctx) and (t0 % 512 == 256):
            i = t0 // 512
            P.op('pool', lambda e, i=i: e.collective_compute("AllGather", ALU.bypass, replica_groups=[[0, 1], [2, 3], [4, 5], [6, 7]],
                                                           ins=[S.AIN[i].ap().opt()], outs=[S.STG[i].ap().opt()]),
                 r=[C.ain_b[i]], w=[C.stg_b[i]], cc=True)


_PROG_CACHE = {}


def _consts():
    GRID_W = 64
    nf = 16
    freqs = np.power(10000.0, -np.arange(nf, dtype=np.float32) / nf).astype(np.float32)
    t = np.arange(8192)
    pos = np.stack([t // GRID_W, t % GRID_W], axis=-1).astype(np.float32)
    ang = pos[:, :, None] * freqs
    cos, sin = np.cos(ang).astype(np.float32), np.sin(ang).astype(np.float32)
    p = np.arange(128)
    axis = (p // 32) % 2
    half = (p // 16) % 2
    f = p % 16
    COS = cos[:, axis, f].T
    SIN = sin[:, axis, f].T * np.where(half == 0, -1.0, 1.0)[:, None]
    rope = np.stack([COS, SIN], axis=1).astype(np.float32)
    perm = np.zeros((128, 128), np.float32)
    perm[p ^ 16, p] = 1.0
    kc = np.arange(64)[:, None]
    qc = np.arange(64)[None, :]
    cstart = np.clip(qc - 8, 0, 48)
    col_ok = (kc >= cstart) & (kc < cstart + 16)
    oh = np.zeros((32, 64, 64), np.float32)
    dc = np.clip(kc - qc + 15, 0, 30)
    for jx in range(31):
        oh[jx] = (dc == jx)
    oh[31] = ~col_ok
    oh = oh.reshape(32, 4096)
    a16 = np.zeros((16, 8, 128), np.float32)
    for c in range(8):
        for pp in range(128):
            a16[2 * c + pp // 64, c, pp] = 1.0
    a16 = a16.reshape(16, 1024)
    rms = []
    for hf in range(2):
        rm = np.zeros((16, 8, 8, 64), np.float32)
        for blk in range(8):
            q0g = hf * 64 + blk * 8
            for rq in range(8):
                qr = q0g + rq
                r0 = min(max(qr - 4, 0), 120)
                for rk in range(16):
                    kr = q0g - 4 + rk
                    ok = (r0 <= kr < r0 + 8)
                    if not ok:
                        rm[rk, blk, rq, :] = NEG
        rms.append(rm.reshape(16, 8 * 512))
    flags = []
    for hf in range(2):
        fl = np.zeros((128, 2), np.float32)
        fl[:, 0] = 1.0 if hf == 1 else 0.0
        fl[:, 1] = 1.0 if hf == 0 else 0.0
        flags.append(fl)
    return dict(rope=rope, perm=perm, oh=oh, a16=a16, rms=rms, flags=flags)


def _fm(v, nch):
    return np.ascontiguousarray(np.asarray(v, np.float32).reshape(nch, 128).T)


def _layer_maps(inp, l, depth=4):
    import math
    vec = np.zeros((128, NVEC), np.float32)
    vec[:, V_BIN:V_BIN + 76] = _fm(inp['b_in'][l], 76)
    vec[:, V_BADA:V_BADA + 48] = _fm(inp['b_ada'][l], 48)
    cw = inp['conv_a_w'][l]
    for k in range(3):
        vec[:, V_CW + k * 4:V_CW + k * 4 + 4] = _fm(cw[k], 4)
    vec[:, V_CB:V_CB + 4] = _fm(inp['conv_a_b'][l], 4)
    dw = inp['conf_dw_w'][l]
    for k in range(31):
        vec[:, V_DW + k * 4:V_DW + k * 4 + 4] = _fm(dw[k], 4)
    vec[:, V_DB:V_DB + 4] = _fm(inp['conf_dw_b'][l], 4)
    vec[:, V_LG:V_LG + 4] = _fm(inp['conf_ln_g'][l], 4)
    vec[:, V_LB:V_LB + 4] = _fm(inp['conf_ln_b'][l], 4)
    vec[:, V_DG] = inp['diff_norm_g'][l]
    for i in range(4):
        vec[:, V_BB + i * 8:V_BB + i * 8 + 8] = _fm(inp['b_branch'][l, i], 8)
    lam_init = 0.8 - 0.6 * math.exp(-0.3 * l)
    vec[:, V_LAM] = lam_init
    vec[:, V_LAM + 1] = 1.0 - lam_init
    rows = np.zeros((1, NROWS), np.float32)
    b_in = inp['b_in'][l]
    b_ada = inp['b_ada'][l]
    rows[0, R_BDV:R_BDV + 512] = b_in[3584:4096]
    rows[0, R_BNV:R_BNV + 512] = b_in[5120:5632]
    rows[0, R_BG1:R_BG1 + 1024] = b_ada[2048:3072]
    rows[0, R_BG2:R_BG2 + 1024] = b_ada[5120:6144]
    rows[0, R_LNG0:R_LNG0 + 1024] = inp['ln_g'][l, 0]
    rows[0, R_LNB0:R_LNB0 + 1024] = inp['ln_b'][l, 0]
    rows[0, R_LNG1:R_LNG1 + 1024] = inp['ln_g'][l, 1]
    rows[0, R_LNB1:R_LNB1 + 1024] = inp['ln_b'][l, 1]
    rows[0, R_LAM:R_LAM + 256] = inp['diff_lambda'][l].reshape(-1)
    rpbT = np.full((32, 120), NEG, np.float32)
    rpbT[0:31, :] = inp['na_rpb'][l].reshape(120, 31).T
    return dict(
        w_ada=inp['w_ada'][l:l + 1], w_in=inp['w_in'][l:l + 1], vec=vec[None], rows=rows[None], rpbT=rpbT[None],
        w_branch=np.ascontiguousarray(inp['w_branch'][l].reshape(1, 2048, DM)), w_o=inp['w_o'][l:l + 1],
        w_ff1=inp['w_ff1'][l:l + 1], w_ff2=inp['w_ff2'][l:l + 1])


def _core_maps(inp, K, x, xc, lws):
    in_maps = []
    for core in range(8):
        b, hf = core // 2, core % 2
        m = dict(lws)
        m['x_own'] = np.ascontiguousarray(x[b, hf * SOWN:(hf + 1) * SOWN])
        m['x_seq'] = np.ascontiguousarray(x[b])
        m['xc'] = np.ascontiguousarray(xc[b])
        cc = np.stack([inp['c'][b], inp['c_ctx']], axis=-1).astype(np.float32)
        m['cc'] = np.ascontiguousarray(cc.reshape(8, 128, 2).transpose(1, 0, 2))
        m['rope_own'] = np.ascontiguousarray(K['rope'][:, :, hf * SOWN:(hf + 1) * SOWN])
        m['rope_seq'] = K['rope']
        m['halo_flag'] = K['flags'][hf]
        m['perm'] = K['perm']
        m['oh'] = K['oh']
        m['a16'] = K['a16']
        m['rm'] = K['rms'][hf]
        in_maps.append(m)
    return in_maps


def kernel(**inp):
    inp = {k: np.asarray(v) for k, v in inp.items()}
    x = np.ascontiguousarray(inp['x'], dtype=np.float32)
    xc = np.ascontiguousarray(inp['ctx'], dtype=np.float32)
    K = _consts()
    depth = 4
    if 'p4' not in _PROG_CACHE:
        _PROG_CACHE['p4'] = build_program(depth, True)
    nc = _PROG_CACHE['p4']
    per = [_layer_maps(inp, l) for l in range(depth)]
    lws = {k: np.ascontiguousarray(np.concatenate([p[k] for p in per], axis=0)) for k in per[0]}
    in_maps = _core_maps(inp, K, x, xc, lws)
    res = run_bass_kernel_spmd(nc, in_maps, core_ids=list(range(8)))
    out = np.empty_like(x)
    for core in range(8):
        b, hf = core // 2, core % 2
        out[b, hf * SOWN:(hf + 1) * SOWN] = res.results[core]['x_out']
    return out
```

```python
import numpy as np
import concourse.bass as bass
import concourse.mybir as mybir
from concourse.bass_utils import run_bass_kernel_spmd
from contextlib import ExitStack

F32 = mybir.dt.float32
BF16 = mybir.dt.bfloat16
AF = mybir.ActivationFunctionType
ALU = mybir.AluOpType
AX = mybir.AxisListType

ENGS = ['pe', 'act', 'dve', 'pool', 'sp']
DMAQ = ('sp', 'pool', 'act')
EPOCH = 12000
NDSEM = {'sp': 40, 'pool': 24, 'act': 8}


class Buf:
    __slots__ = ('name', 'w', 'r', 'prev', 'open')

    def __init__(self, name):
        self.name = name
        self.w = {}
        self.r = {}
        self.prev = set()
        self.open = False


class Op:
    __slots__ = ('eng', 'fn', 'deps', 'dma', 'sig', 'dsem', 'dval', 'signals', 'cc')

    def __init__(self, eng, fn, deps, dma):
        self.eng = eng
        self.fn = fn
        self.deps = deps
        self.dma = dma
        self.sig = None
        self.dsem = None
        self.dval = None
        self.signals = False
        self.cc = False


class Prog:
    def __init__(self, nc):
        self.nc = nc
        self.ops = []
        self.eng_ops = {e: [] for e in ENGS}
        self.dcount = {q: 0 for q in DMAQ}
        self.dlast = {q: {} for q in DMAQ}
        self.final_dmas = []
        self.ncc = 0

    def buf(self, name):
        return Buf(name)

    def _key(self, oid):
        o = self.ops[oid]
        return ('d', oid) if o.dma else o.eng

    def op(self, eng, fn, r=(), w=(), wacc=(), dma=False, cc=False):
        oid = len(self.ops)
        deps = set()
        for b in r:
            deps.update(b.w.values())
        for b in w:
            deps.update(b.w.values())
            deps.update(b.r.values())
        for b in wacc:
            if b.open:
                deps.update(b.prev)
            else:
                deps.update(b.w.values())
                deps.update(b.r.values())
        o = Op(eng, fn, deps, dma)
        if cc:
            o.cc = True
            o.dma = True
            self.ncc += 1
            o.dsem = 'cc'
            o.dval = self.ncc
            self.last_cc = oid
        elif dma:
            q = eng
            slot = self.dcount[q] % NDSEM[q]
            o.dsem = slot
            o.dval = 16 * (self.dcount[q] // NDSEM[q] + 1)
            if slot in self.dlast[q]:
                deps.add(self.dlast[q][slot])
            self.dlast[q][slot] = oid
            self.dcount[q] += 1
        self.ops.append(o)
        self.eng_ops[eng].append(oid)
        key = ('d', oid) if (dma or cc) else eng
        for b in r:
            b.r[key] = oid
            b.open = False
        for b in w:
            b.w = {key: oid}
            b.r = {}
            b.open = False
            b.prev = set()
        for b in wacc:
            if b.open:
                b.w[key] = oid
            else:
                b.prev = set(b.w.values()) | set(b.r.values())
                b.w = {key: oid}
                b.r = {}
                b.open = True
        return oid

    def pe(self, fn, r=(), w=(), wacc=()):
        return self.op('pe', fn, r, w, wacc)

    def act(self, fn, r=(), w=(), wacc=()):
        return self.op('act', fn, r, w, wacc)

    def dve(self, fn, r=(), w=(), wacc=()):
        return self.op('dve', fn, r, w, wacc)

    def pool(self, fn, r=(), w=(), wacc=()):
        return self.op('pool', fn, r, w, wacc)

    def dma(self, q, out, in_, r=(), w=(), wacc=(), final=False, **kw):
        oid = self.op(q, lambda e: e.dma_start(out=out, in_=in_, **kw), r, w, wacc, dma=True)
        if final:
            self.final_dmas.append(oid)
        return oid

    def emit(self, stack):
        nc = self.nc
        ops = self.ops
        for o in ops:
            for d in o.deps:
                do = ops[d]
                if do.dma:
                    continue
                if do.eng == o.eng and o.eng == 'pe':
                    continue
                do.signals = True
        nsig = {e: 0 for e in ENGS}
        for e in ENGS:
            for oid in self.eng_ops[e]:
                o = ops[oid]
                if (not o.dma) and o.signals:
                    nsig[e] += 1
                    o.sig = nsig[e]
        esems = {}
        for e in ENGS:
            n = (nsig[e] + EPOCH - 1) // EPOCH
            esems[e] = [stack.enter_context(nc.semaphore(f"s_{e}_{i}")) for i in range(max(n, 1))]
        dsems = {q: [stack.enter_context(nc.semaphore(f"d_{q}_{i}")) for i in range(min(NDSEM[q], self.dcount[q]))]
                 for q in DMAQ}
        self.nsig = nsig
        ccsem = stack.enter_context(nc.semaphore('cc_sem')) if self.ncc else None
        handles = {'pe': 'tensor', 'act': 'scalar', 'dve': 'vector', 'pool': 'gpsimd', 'sp': 'sync'}

        def emit_eng(ename, e):
            seen = {}
            for oid in self.eng_ops[ename]:
                o = ops[oid]
                need = {}
                for d in o.deps:
                    do = ops[d]
                    if do.cc:
                        k = ('c',)
                        v = do.dval
                    elif do.dma:
                        k = ('d', do.eng, do.dsem)
                        v = do.dval
                    else:
                        if do.eng == ename and ename == 'pe':
                            continue
                        k = ('e', do.eng)
                        v = do.sig
                    if v > need.get(k, 0):
                        need[k] = v
                for k, v in need.items():
                    if seen.get(k, 0) >= v:
                        continue
                    seen[k] = v
                    if k[0] == 'c':
                        e.wait_ge(ccsem, v)
                    elif k[0] == 'd':
                        e.wait_ge(dsems[k[1]][k[2]], v)
                    else:
                        ep = (v - 1) // EPOCH
                        e.wait_ge(esems[k[1]][ep], v - ep * EPOCH)
                ins = o.fn(e)
                if o.cc:
                    ins.then_inc(ccsem)
                elif o.dma:
                    ins.then_inc(dsems[ename][o.dsem], 16)
                elif o.signals:
                    ep = (o.sig - 1) // EPOCH
                    ins.then_inc(esems[ename][ep], 1)
            if ename == 'sp':
                for oid in self.final_dmas:
                    do = ops[oid]
                    e.wait_ge(dsems[do.eng][do.dsem], do.dval)

        block = stack.enter_context(nc.Block())
        for ename in ENGS:
            dec = getattr(block, handles[ename])

            def mk(ename):
                def body(e):
                    emit_eng(ename, e)
                return body
            dec(mk(ename))

DM = 1024
SOWN = 4096
CTXL = 256
NTOK = SOWN + CTXL
ALPHA = 8.0 ** 0.25
LN_EPS = 1e-5
NEG = -1.0e4
V_BIN, V_BADA, V_CW, V_CB, V_DW, V_DB, V_LG, V_LB, V_DG, V_BB, V_LAM = 0, 76, 124, 136, 140, 264, 268, 272, 276, 277, 309
NVEC = 311
R_BDV, R_BNV, R_BG1, R_BG2, R_LNG0, R_LNB0, R_LNG1, R_LNB1, R_LAM = 0, 512, 1024, 2048, 3072, 4096, 5120, 6144, 7168
NROWS = 7424
ARENA_WORDS = 51200


class T:
    __slots__ = ('ap', 'b')

    def __init__(self, ap, b):
        self.ap = ap
        self.b = b

    def __getitem__(self, k):
        return self.ap[k]


class Arena:
    def __init__(self, ar):
        self.ar = ar
        self.off = 0

    def alloc(self, name, free, dtype, parts=128):
        n = 1
        for f in free:
            n *= f
        words = n if dtype == F32 else (n + 1) // 2
        words = (words + 7) // 8 * 8
        assert self.off + words <= ARENA_WORDS, (name, self.off, words)
        ap = self.ar[0:parts, self.off:self.off + words]
        if dtype != F32:
            ap = ap.bitcast(dtype)
        ap = ap[:, 0:n]
        if len(free) == 2:
            ap = ap.rearrange("p (a b) -> p a b", a=free[0])
        elif len(free) == 3:
            ap = ap.rearrange("p (a b c) -> p a b c", a=free[0], b=free[1])
        self.off += words
        return T(ap, Buf(name))

    def ring(self, name, n, free, dtype, parts=128):
        return Ring([self.alloc(f"{name}{i}", free, dtype, parts) for i in range(n)])


class Ring:
    def __init__(self, items):
        self.items = items
        self.i = 0

    def next(self):
        t = self.items[self.i % len(self.items)]
        self.i += 1
        return t


def ACT(P, out, in_, func, r, w=(), wacc=(), **kw):
    return P.op('act', lambda e: e.activation(out=out, in_=in_, func=func, **kw), r, w, wacc)


def TT(P, eng, out, in0, in1, op, r, w=(), wacc=()):
    return P.op(eng, lambda e: e.tensor_tensor(out=out, in0=in0, in1=in1, op=op), r, w, wacc)


def TS(P, eng, out, in0, s1, s2, op0, op1, r, w=(), wacc=()):
    if op1 is None:
        return P.op(eng, lambda e: e.tensor_scalar(out=out, in0=in0, scalar1=s1, scalar2=None, op0=op0), r, w, wacc)
    return P.op(eng, lambda e: e.tensor_scalar(out=out, in0=in0, scalar1=s1, scalar2=s2, op0=op0, op1=op1), r, w, wacc)


def STT(P, eng, out, in0, scalar, in1, op0, op1, r, w=(), wacc=()):
    return P.op(eng, lambda e: e.scalar_tensor_tensor(out=out, in0=in0, scalar=scalar, in1=in1, op0=op0, op1=op1), r, w, wacc)


def MM(P, bank, out, lhsT, rhs, start, stop, r):
    fn = lambda e: e.matmul(out, lhsT, rhs, start=start, stop=stop)
    if start:
        return P.op('pe', fn, r, w=[bank.b])
    return P.op('pe', fn, r, wacc=[bank.b])


def TR(P, bank, out, in_, ident, r):
    return P.op('pe', lambda e: e.transpose(out, in_, ident), r, w=[bank.b])


def load_w(P, dst, src, k_chunks, c0, c1, wbuf, q='pool'):
    for k in range(k_chunks):
        for a in range(c0, c1, 2048):
            b = min(c1, a + 2048)
            P.dma(q, dst[:, k, a - c0:b - c0], src[k * 128:(k + 1) * 128, a:b], wacc=[wbuf])


class Ctx:
    pass


def build_program(n_layers, fused):
    nc = bass.Bass("TRN2", target_bir_lowering=False)
    C = Ctx()
    C.nc = nc
    dt = nc.dram_tensor

    def din(name, shape, dtype=F32):
        return dt(name, list(shape), dtype, kind="ExternalInput").ap()

    def dsc(name, shape, dtype=BF16):
        return dt(name, list(shape), dtype, kind="Internal").ap()

    L = n_layers
    I = Ctx()
    I.x_own = din("x_own", [SOWN, DM])
    I.x_seq = din("x_seq", [2 * SOWN, DM])
    I.xc = din("xc", [CTXL, DM])
    I.cc = din("cc", [128, 8, 2])
    I.w_ada = din("w_ada", [L, DM, 6 * DM])
    I.w_in = din("w_in", [L, DM, 9728])
    I.vec = din("vec", [L, 128, NVEC])
    I.rows = din("rows", [L, 1, NROWS])
    I.rpbT = din("rpbT", [L, 32, 120])
    I.w_branch = din("w_branch", [L, 2048, DM])
    I.w_o = din("w_o", [L, DM, DM])
    I.w_ff1 = din("w_ff1", [L, DM, 4 * DM])
    I.w_ff2 = din("w_ff2", [L, 4 * DM, DM])
    I.rope_own = din("rope_own", [128, 2, SOWN])
    I.rope_seq = din("rope_seq", [128, 2, 2 * SOWN])
    I.halo_flag = din("halo_flag", [128, 2])
    I.perm = din("perm", [128, 128])
    I.oh = din("oh", [32, 4096])
    I.a16 = din("a16", [16, 8 * 128])
    I.rm = din("rm", [16, 8 * 512])
    O = Ctx()
    O.x_out = dt("x_out", [SOWN, DM], F32, kind="ExternalOutput").ap()
    O.xc_out = dt("xc_out", [CTXL, DM], F32, kind="ExternalOutput").ap()
    S = Ctx()
    S.UT = dsc("s_ut", [128, 8, NTOK])
    S.CONV = dsc("s_conv", [3, 128, 4, SOWN + 32])
    S.CONVC = dsc("s_convc", [3, 128, 4, CTXL + 32])
    S.QD = dsc("s_qd", [128, 4, NTOK])
    S.KD = dsc("s_kd", [128, 4, 8448])
    S.VD = dsc("s_vd", [8448, 512])
    S.QN = dsc("s_qn", [128, 4, NTOK])
    S.KN = dsc("s_kn", [128, 4, SOWN + 512])
    S.KNC = dsc("s_knc", [128, 4, CTXL])
    S.VN = dsc("s_vn", [SOWN + 512, 512])
    S.VNC = dsc("s_vnc", [CTXL, 512])
    S.Y = dsc("s_y", [128, 16, NTOK])
    S.X1 = dsc("s_x1", [NTOK, DM], F32)
    S.RPBF = dsc("s_rpbf", [8, 15, 64, 64])
    S.GB = dsc("s_gb", [4, 128, DM], F32)
    S.XC = dsc("s_xc", [2, CTXL, DM], F32)
    S.AIN = [dt(f"cc_in{i}", [512, DM], F32, kind="Internal") for i in range(8)] if n_layers > 1 else None
    S.STG = [dt(f"cc_out{i}", [1024, DM], F32, kind="Internal") for i in range(8)] if n_layers > 1 else None
    C.I, C.O, C.S = I, O, S

    with ExitStack() as st:
        P = Prog(nc)
        C.P = P
        ar = st.enter_context(nc.sbuf_tensor("arena", [128, ARENA_WORDS], F32))
        A = Arena(ar)
        C.A = A
        C.ps = [T(st.enter_context(nc.psum_tensor(f"ps{i}", [128, 512], F32))[:, :], Buf(f"ps{i}")) for i in range(8)]
        C.dummy = Buf("dummy")
        setup_persistent(C)
        base = A.off
        C.ain_b = [Buf(f"ain{i}") for i in range(8)]
        C.stg_b = [Buf(f"stg{i}") for i in range(8)]
        for l in range(n_layers):
            C.l = l
            last_prog_layer = (l == n_layers - 1)
            C.final = last_prog_layer
            if l == 0:
                C.own_rows = lambda t0, n: I.x_own[t0:t0 + n, :]
                C.glob_tile = lambda gi: I.x_seq[gi * 512:(gi + 1) * 512, :]
                C.xin_ctx = I.xc
            else:
                C.own_rows = lambda t0, n: S.AIN[t0 // 512].ap()[t0 % 512:t0 % 512 + n, :]
                C.glob_tile = lambda gi: S.STG[gi % 8].ap()[(gi // 8) * 512:(gi // 8 + 1) * 512, :]
                C.xin_ctx = S.XC[(l - 1) % 2]
            if last_prog_layer:
                C.out_rows = lambda t0, n: O.x_out[t0:t0 + n, :]
                C.xo_ctx = O.xc_out
            else:
                C.out_rows = lambda t0, n: S.AIN[t0 // 512].ap()[t0 % 512:t0 % 512 + n, :]
                C.xo_ctx = S.XC[l % 2]
            for ph in (phase0, phase1, phase2a, phase2b, phase2c, phase3, phase4):
                A.off = base
                P.barrier()
                ph(C)
        P.emit(st)
        global _LAST_PROG
        _LAST_PROG = P
    return nc


def setup_persistent(C):
    P, A, I = C.P, C.A, C.I
    C.ident_f = A.alloc("ident_f", [128], F32)
    C.ident8 = A.alloc("ident8", [128], BF16)
    C.ones_bf = A.alloc("ones_bf", [128], BF16)
    C.ones_f = A.alloc("ones_f", [128], F32)
    C.perm = A.alloc("perm", [128], BF16)
    C.vec = A.alloc("vec", [NVEC], F32)
    C.modfm = A.alloc("modfm", [48, 2], F32)
    C.flag = A.alloc("flag", [2], F32)
    C.lam = A.alloc("lam", [4], F32)
    C.gs = A.alloc("gs", [1], F32)
    idf = C.ident_f
    P.pool(lambda e: e.memset(idf[:, :], 0.0), w=[idf.b])
    P.pool(lambda e: e.affine_select(out=idf[:, :], in_=idf[:, :], pattern=[[-1, 128]], compare_op=ALU.not_equal,
                                     fill=1.0, base=0, channel_multiplier=1), r=[idf.b], w=[idf.b])
    TS(P, 'dve', C.ident8[:, :], idf[:, :], 8.0, None, ALU.mult, None, r=[idf.b], w=[C.ident8.b])
    ob, of = C.ones_bf, C.ones_f
    P.pool(lambda e: e.memset(ob[:, :], 1.0), w=[ob.b])
    P.pool(lambda e: e.memset(of[:, :], 1.0), w=[of.b])
    P.dma('pool', C.perm[:, :], I.perm[:, :], w=[C.perm.b])
    P.dma('sp', C.flag[:, :], I.halo_flag[:, :], w=[C.flag.b])


def barrier(self):
    pend = set()
    for e in ENGS:
        if self.eng_ops[e]:
            last_c = None
            for oid in reversed(self.eng_ops[e]):
                if not self.ops[oid].dma:
                    last_c = oid
                    break
            if last_c is not None:
                pend.add(last_c)
    for q in DMAQ:
        pend.update(self.dlast[q].values())
    if getattr(self, 'last_cc', None) is not None:
        pend.add(self.last_cc)
    self.pending = {e: set(pend) for e in ENGS}


Prog.barrier = barrier
_orig_op = Prog.op


def _op(self, eng, fn, r=(), w=(), wacc=(), dma=False, cc=False):
    oid = _orig_op(self, eng, fn, r, w, wacc, dma, cc)
    pend = getattr(self, 'pending', None)
    if pend and pend.get(eng):
        self.ops[oid].deps.update(x for x in pend[eng] if x != oid)
        pend[eng] = None
    return oid


Prog.op = _op


def bcast_row(C, a, n):
    return C.I.rows[C.l, 0, a:a + n].partition_broadcast(128)


def phase0(C):
    P, A, I, S, l = C.P, C.A, C.I, C.S, C.l
    psr = Ring(C.ps)
    P.dma('sp', C.vec[:, :], I.vec[l], w=[C.vec.b])
    dl = A.alloc("dl", [256], F32)
    P.dma('sp', dl[:, :], bcast_row(C, R_LAM, 256), w=[dl.b])
    dlv = dl.ap.rearrange("p (a b c) -> p a b c", a=2, b=2)
    prod = A.alloc("prod", [2, 64], F32)
    TT(P, 'dve', prod[:, :, :], dlv[:, :, 0, :], dlv[:, :, 1, :], ALU.mult, r=[dl.b], w=[prod.b])
    s2 = A.alloc("s2", [2], F32)
    P.dve(lambda e: e.reduce_sum(out=s2[:, :], in_=prod[:, :, :], axis=AX.X), r=[prod.b], w=[s2.b])
    e2 = A.alloc("e2", [2], F32)
    ACT(P, e2[:, :], s2[:, :], AF.Exp, r=[s2.b], w=[e2.b])
    TT(P, 'dve', C.lam[:, 1:2], e2[:, 0:1], e2[:, 1:2], ALU.subtract, r=[e2.b], w=[C.lam.b])
    TS(P, 'dve', C.lam[:, 0:1], C.lam[:, 1:2], C.vec[:, V_LAM:V_LAM + 1], -1.0, ALU.add, ALU.mult,
       r=[C.lam.b, C.vec.b], w=[C.lam.b])
    TT(P, 'dve', C.gs[:, :], C.vec[:, V_DG:V_DG + 1], C.vec[:, V_LAM + 1:V_LAM + 2], ALU.mult, r=[C.vec.b], w=[C.gs.b])
    ccs = A.alloc("ccs", [8, 2], F32)
    P.dma('sp', ccs[:, :, :], I.cc[:, :, :], w=[ccs.b])
    scs = A.alloc("scs", [8, 2], F32)
    ACT(P, scs[:, :, :], ccs[:, :, :], AF.Silu, r=[ccs.b], w=[scs.b])
    scsb = A.alloc("scsb", [8, 2], BF16)
    P.dve(lambda e: e.tensor_copy(out=scsb[:, :, :], in_=scs[:, :, :]), r=[scs.b], w=[scsb.b])
    SCB = A.alloc("SCB", [2, 8, 128], BF16)
    for j in range(2):
        for k in range(8):
            ACT(P, SCB[:, j, k, :], C.ones_f[:, :], AF.Identity, r=[scs.b, C.ones_f.b], wacc=[SCB.b],
                scale=scs[:, k, j:j + 1])
    bg = A.alloc("bg", [2, 1024], F32)
    P.dma('sp', bg[:, 0, :], bcast_row(C, R_BG1, 1024), wacc=[bg.b])
    P.dma('sp', bg[:, 1, :], bcast_row(C, R_BG2, 1024), wacc=[bg.b])
    gbt = A.alloc("gbt", [4, 1024], F32)
    war = A.ring("wa", 2, [8, 512], BF16)
    for s in range(12):
        wa = war.next()
        load_w(P, wa, I.w_ada[l], 8, s * 512, (s + 1) * 512, wa.b)
        for cc in range(4):
            chunk = s * 4 + cc
            ps = psr.next()
            for k in range(8):
                MM(P, ps, ps[:, 0:2], wa[:, k, cc * 128:(cc + 1) * 128], scsb[:, k, :], k == 0, k == 7, r=[wa.b, scsb.b])
            TS(P, 'dve', C.modfm[:, chunk, :], ps[:, 0:2], C.vec[:, V_BADA + chunk:V_BADA + chunk + 1], None, ALU.add, None,
               r=[ps.b, C.vec.b], wacc=[C.modfm.b])
        if s in (4, 5, 10, 11):
            which = 0 if s < 6 else 1
            col0 = (s - (4 if s < 6 else 10)) * 512
            for j in range(2):
                ps = psr.next()
                for k in range(8):
                    MM(P, ps, ps[:, :], SCB[:, j, k, :], wa[:, k, :], k == 0, k == 7, r=[wa.b, SCB.b])
                TT(P, 'dve', gbt[:, which * 2 + j, col0:col0 + 512], ps[:, :], bg[:, which, col0:col0 + 512], ALU.add,
                   r=[ps.b, bg.b], wacc=[gbt.b])
    TS(P, 'dve', C.modfm[:, 8:16, :], C.modfm[:, 8:16, :], 1.0, None, ALU.add, None, r=[C.modfm.b], w=[C.modfm.b])
    TS(P, 'dve', C.modfm[:, 32:40, :], C.modfm[:, 32:40, :], 1.0, None, ALU.add, None, r=[C.modfm.b], w=[C.modfm.b])
    P.dma('sp', S.GB.rearrange("g p d -> p g d"), gbt[:, :, :], r=[gbt.b], w=[C.dummy])
    rp = A.alloc("rp", [120], F32, parts=32)
    oh = A.alloc("oh", [4096], F32, parts=32)
    rb = A.alloc("rb", [4096], BF16, parts=120)
    P.dma('sp', rp[:, :], I.rpbT[l], w=[rp.b])
    P.dma('sp', oh[:, :], I.oh[:, :], w=[oh.b])
    for s in range(8):
        ps = psr.next()
        MM(P, ps, ps[0:120, :], rp[:, :], oh[:, s * 512:(s + 1) * 512], True, True, r=[rp.b, oh.b])
        ACT(P, rb[:, s * 512:(s + 1) * 512], ps[0:120, :], AF.Identity, r=[ps.b], wacc=[rb.b])
    P.dma('sp', S.RPBF.rearrange("h d k q -> (h d) (k q)"), rb[:, :], r=[rb.b], w=[C.dummy])
    if l == 0:
        zt = A.alloc("zt", [3, 4, CTXL + 32], BF16)
        P.pool(lambda e: e.memset(zt[:, :, :, :], 0.0), w=[zt.b])
        P.dma('sp', S.CONVC.rearrange("a p c t -> p a c t"), zt[:, :, :, :], r=[zt.b], w=[C.dummy])


def phase1(C):
    P, A, I, S, l = C.P, C.A, C.I, C.S, C.l
    psr = Ring(C.ps)
    W = A.alloc("w1in", [8, 5632], BF16)
    load_w(P, W, I.w_in[l], 8, 0, 5632, W.b)
    bdv = A.alloc("bdv", [2, 512], F32)
    P.dma('sp', bdv[:, 0, :], bcast_row(C, R_BDV, 512), wacc=[bdv.b])
    P.dma('sp', bdv[:, 1, :], bcast_row(C, R_BNV, 512), wacc=[bdv.b])
    xt = A.alloc("xt", [4, 1024], F32)
    uTr = A.ring("uT", 2, [8, 512], BF16)
    ropr = A.ring("rope", 2, [2, 512], F32)
    tf = A.ring("tf", 4, [512], F32)
    tb = A.ring("tb", 6, [512], BF16)
    qbr = A.ring("qb", 2, [512], BF16)
    vec, dummy = C.vec, C.dummy

    def bias(chunk):
        return vec[:, V_BIN + chunk:V_BIN + chunk + 1]

    def tile(kind, ti):
        N = 256 if kind == 'ctx' else 512
        nsub = N // 128
        j = 1 if kind == 'ctx' else 0
        t0 = 0 if kind == 'ctx' else ti * 512
        if kind == 'own':
            src = C.own_rows(t0, N)
            rdeps = [C.ain_b[ti]]
        elif kind == 'glob':
            src = C.glob_tile(ti)
            rdeps = [C.stg_b[ti % 8]]
        else:
            src = C.xin_ctx[0:N, :]
            rdeps = []
        P.dma('sp', xt[:, 0:nsub, :], src.rearrange("(j p) f -> p j f", p=128), r=rdeps, w=[xt.b])
        uT = uTr.next()
        for k in range(8):
            ps = psr.next()
            for jj in range(nsub):
                TR(P, ps, ps[:, jj * 128:(jj + 1) * 128], xt[:, jj, k * 128:(k + 1) * 128], C.ident_f[:, :],
                   r=[xt.b, C.ident_f.b])
            ACT(P, uT[:, k, 0:N], ps[:, 0:N], AF.Identity, r=[ps.b, C.modfm.b], wacc=[uT.b],
                scale=C.modfm[:, 8 + k, j:j + 1], bias=C.modfm[:, k, j:j + 1])
        rope = kind != 'ctx'
        if kind != 'glob':
            tok0 = t0 if kind == 'own' else SOWN
            P.dma('sp', S.UT[:, :, tok0:tok0 + N], uT[:, :, 0:N], r=[uT.b], w=[dummy])
        if rope:
            rp = ropr.next()
            rsrc = I.rope_own if kind == 'own' else I.rope_seq
            P.dma('sp', rp[:, :, :], rsrc[:, :, t0:t0 + 512], w=[rp.b])

        def fm(chunk, n0=0, n1=N):
            ps = psr.next()
            for k in range(8):
                MM(P, ps, ps[:, 0:n1 - n0], W[:, k, chunk * 128:(chunk + 1) * 128], uT[:, k, n0:n1], k == 0, k == 7,
                   r=[W.b, uT.b])
            return ps

        def plain(chunk, dst_ap, n0=0, n1=N, func=AF.Identity):
            ps = fm(chunk, n0, n1)
            o = tb.next()
            n = n1 - n0
            ACT(P, o[:, 0:n], ps[:, 0:n], func, r=[ps.b, vec.b], w=[o.b], bias=bias(chunk))
            P.dma('sp', dst_ap, o[:, 0:n], r=[o.b], w=[dummy])

        def prod2(chunk_a, func_a, chunk_b, dst_ap, n0=0, n1=N, flag=None):
            n = n1 - n0
            ps1 = fm(chunk_a, n0, n1)
            a = tf.next()
            ACT(P, a[:, 0:n], ps1[:, 0:n], func_a, r=[ps1.b, vec.b], w=[a.b], bias=bias(chunk_a))
            ps2 = fm(chunk_b, n0, n1)
            o = tb.next()
            if flag is None:
                STT(P, 'dve', o[:, 0:n], ps2[:, 0:n], bias(chunk_b), a[:, 0:n], ALU.add, ALU.mult,
                    r=[ps2.b, a.b, vec.b], w=[o.b])
            else:
                a2 = tf.next()
                STT(P, 'dve', a2[:, 0:n], ps2[:, 0:n], bias(chunk_b), a[:, 0:n], ALU.add, ALU.mult,
                    r=[ps2.b, a.b, vec.b], w=[a2.b])
                TS(P, 'dve', o[:, 0:n], a2[:, 0:n], C.flag[:, flag:flag + 1], None, ALU.mult, None,
                   r=[a2.b, C.flag.b], w=[o.b])
            P.dma('sp', dst_ap, o[:, 0:n], r=[o.b], w=[dummy])

        def roped(chunk, dst_ap):
            ps = fm(chunk)
            if not rope:
                o = tb.next()
                ACT(P, o[:, 0:N], ps[:, 0:N], AF.Identity, r=[ps.b, vec.b], w=[o.b], bias=bias(chunk))
            else:
                qb = qbr.next()
                ACT(P, qb[:, 0:N], ps[:, 0:N], AF.Identity, r=[ps.b, vec.b], w=[qb.b], bias=bias(chunk))
                ps2 = psr.next()
                MM(P, ps2, ps2[:, 0:N], C.perm[:, :], qb[:, 0:N], True, True, r=[C.perm.b, qb.b])
                t1 = tf.next()
                TT(P, 'dve', t1[:, 0:N], qb[:, 0:N], rp[:, 0, 0:N], ALU.mult, r=[qb.b, rp.b], w=[t1.b])
                t2 = tf.next()
                TT(P, 'dve', t2[:, 0:N], ps2[:, 0:N], rp[:, 1, 0:N], ALU.mult, r=[ps2.b, rp.b], w=[t2.b])
                o = tb.next()
                TT(P, 'dve', o[:, 0:N], t1[:, 0:N], t2[:, 0:N], ALU.add, r=[t1.b, t2.b], w=[o.b])
            P.dma('sp', dst_ap, o[:, 0:N], r=[o.b], w=[dummy])

        def tm(col0, bi, dst, rowoff, subs):
            for jj in subs:
                ps = psr.next()
                for k in range(8):
                    MM(P, ps, ps[:, :], uT[:, k, jj * 128:(jj + 1) * 128], W[:, k, col0:col0 + 512], k == 0, k == 7,
                       r=[W.b, uT.b])
                o = tb.next()
                TT(P, 'dve', o[:, :], ps[:, :], bdv[:, bi, :], ALU.add, r=[ps.b, bdv.b], w=[o.b])
                P.dma('sp', dst[rowoff + jj * 128:rowoff + (jj + 1) * 128, :], o[:, :], r=[o.b], w=[dummy])

        if kind in ('own', 'ctx'):
            cv = S.CONV if kind == 'own' else S.CONVC
            co = 16 + t0
            qo = t0 if kind == 'own' else SOWN
            ko = 256 + t0 if kind == 'own' else 0
            for c in range(4):
                plain(c, cv[0, :, c, co:co + N])
                prod2(4 + c, AF.Identity, 8 + c, cv[1, :, c, co:co + N])
                prod2(16 + c, AF.Sigmoid, 12 + c, cv[2, :, c, co:co + N])
            for c in range(4):
                roped(20 + c, S.QD[:, c, qo:qo + N])
                if kind == 'ctx':
                    roped(24 + c, S.KD[:, c, 0:N])
            for c in range(4):
                plain(32 + c, S.QN[:, c, qo:qo + N])
                if kind == 'own':
                    plain(36 + c, S.KN[:, c, ko:ko + N])
                else:
                    plain(36 + c, S.KNC[:, c, 0:N])
            if kind == 'own':
                tm(5120, 1, S.VN, ko, range(nsub))
            else:
                tm(3584, 0, S.VD, 0, range(nsub))
                tm(5120, 1, S.VNC, 0, range(nsub))
        else:
            ko = 256 + t0
            for c in range(4):
                roped(24 + c, S.KD[:, c, ko:ko + N])
            tm(3584, 0, S.VD, ko, range(nsub))
            if ti == 8:
                for c in range(4):
                    plain(36 + c, S.KN[:, c, 4352:4608], 0, 256)
                    prod2(4 + c, AF.Identity, 8 + c, S.CONV[1, :, c, 4112:4128], 0, 16, flag=1)
                    prod2(16 + c, AF.Sigmoid, 12 + c, S.CONV[2, :, c, 4112:4128], 0, 16, flag=1)
                tm(5120, 1, S.VN, 4352, range(0, 2))
            if ti == 7:
                for c in range(4):
                    plain(36 + c, S.KN[:, c, 0:256], 256, 512)
                    prod2(4 + c, AF.Identity, 8 + c, S.CONV[1, :, c, 0:16], 496, 512, flag=0)
                    prod2(16 + c, AF.Sigmoid, 12 + c, S.CONV[2, :, c, 0:16], 496, 512, flag=0)
                tm(5120, 1, S.VN, -256, range(2, 4))

    for ti in range(8):
        tile('own', ti)
    tile('ctx', 0)
    for ti in range(16):
        tile('glob', ti)


def phase2a(C):
    P, A, I, S, l = C.P, C.A, C.I, C.S, C.l
    psr = Ring(C.ps)
    vec, dummy = C.vec, C.dummy
    D3 = A.alloc("D3", [12, 128], BF16)
    D31 = A.alloc("D31", [124, 128], BF16)
    for i in range(12):
        TS(P, 'pool' if i % 2 else 'dve', D3[:, i, :], C.ident_f[:, :], vec[:, V_CW + i:V_CW + i + 1], None, ALU.mult, None,
           r=[C.ident_f.b, vec.b], wacc=[D3.b])
    for i in range(124):
        TS(P, 'pool' if i % 2 else 'dve', D31[:, i, :], C.ident_f[:, :], vec[:, V_DW + i:V_DW + i + 1], None, ALU.mult, None,
           r=[C.ident_f.b, vec.b], wacc=[D31.b])
    abr = A.ring("ab", 2, [4, 512], BF16)
    cxr = A.ring("cx", 2, [4, 514], BF16)
    glr = A.ring("gl", 2, [4, 542], BF16)
    z = A.alloc("z", [4, 512], F32)
    zsq = A.alloc("zsq", [4, 512], F32)
    mean = A.alloc("mean", [512], F32)
    msq = A.alloc("msq", [512], F32)
    var = A.alloc("var", [512], F32)
    rstd = A.alloc("rstd", [512], F32)
    tf = A.ring("tf", 3, [512], F32)
    tb = A.ring("tb", 4, [512], BF16)
    tiles = [('own', ti) for ti in range(8)] + [('ctx', 0)]
    for kind, ti in tiles:
        N = 512 if kind == 'own' else 256
        cv = S.CONV if kind == 'own' else S.CONVC
        co = 16 + ti * 512
        yo = ti * 512 if kind == 'own' else SOWN
        ab, cx, gl = abr.next(), cxr.next(), glr.next()
        P.dma('sp', ab[:, :, 0:N], cv[0, :, :, co:co + N], w=[ab.b])
        P.dma('sp', cx[:, :, 0:N + 2], cv[1, :, :, co - 1:co + N + 1], w=[cx.b])
        P.dma('sp', gl[:, :, 0:N + 30], cv[2, :, :, co - 15:co + N + 15], w=[gl.b])
        for c in range(4):
            ps = psr.next()
            for k in range(3):
                MM(P, ps, ps[:, 0:N], D3[:, k * 4 + c, :], cx[:, c, k:k + N], k == 0, k == 2, r=[D3.b, cx.b])
            o = tb.next()
            STT(P, 'dve', o[:, 0:N], ps[:, 0:N], vec[:, V_CB + c:V_CB + c + 1], ab[:, c, 0:N], ALU.add, ALU.mult,
                r=[ps.b, ab.b, vec.b], w=[o.b])
            P.dma('sp', S.Y[:, c, yo:yo + N], o[:, 0:N], r=[o.b], w=[dummy])
        for c in range(4):
            ps = psr.next()
            for k in range(31):
                MM(P, ps, ps[:, 0:N], D31[:, k * 4 + c, :], gl[:, c, k:k + N], k == 0, k == 30, r=[D31.b, gl.b])
            ACT(P, z[:, c, 0:N], ps[:, 0:N], AF.Identity, r=[ps.b, vec.b], wacc=[z.b], bias=vec[:, V_DB + c:V_DB + c + 1])
            ACT(P, zsq[:, c, 0:N], ps[:, 0:N], AF.Square, r=[ps.b, vec.b], wacc=[zsq.b], bias=vec[:, V_DB + c:V_DB + c + 1])
        psm = psr.next()
        for c in range(4):
            MM(P, psm, psm[:, 0:N], C.ones_f[:, :], z[:, c, 0:N], c == 0, c == 3, r=[C.ones_f.b, z.b])
        pss = psr.next()
        for c in range(4):
            MM(P, pss, pss[:, 0:N], C.ones_f[:, :], zsq[:, c, 0:N], c == 0, c == 3, r=[C.ones_f.b, zsq.b])
        ACT(P, mean[:, 0:N], psm[:, 0:N], AF.Identity, r=[psm.b], w=[mean.b], scale=1.0 / 512)
        TT(P, 'dve', msq[:, 0:N], mean[:, 0:N], mean[:, 0:N], ALU.mult, r=[mean.b], w=[msq.b])
        STT(P, 'dve', var[:, 0:N], pss[:, 0:N], 1.0 / 512, msq[:, 0:N], ALU.mult, ALU.subtract, r=[pss.b, msq.b], w=[var.b])
        TS(P, 'dve', var[:, 0:N], var[:, 0:N], LN_EPS, None, ALU.add, None, r=[var.b], w=[var.b])
        ACT(P, var[:, 0:N], var[:, 0:N], AF.Sqrt, r=[var.b], w=[var.b])
        P.dve(lambda e, N=N: e.reciprocal(out=rstd[:, 0:N], in_=var[:, 0:N]), r=[var.b], w=[rstd.b])
        for c in range(4):
            t = tf.next()
            TT(P, 'dve', t[:, 0:N], z[:, c, 0:N], mean[:, 0:N], ALU.subtract, r=[z.b, mean.b], w=[t.b])
            TT(P, 'dve', t[:, 0:N], t[:, 0:N], rstd[:, 0:N], ALU.mult, r=[t.b, rstd.b], w=[t.b])
            o = tb.next()
            ACT(P, o[:, 0:N], t[:, 0:N], AF.Silu, r=[t.b, vec.b], w=[o.b], scale=vec[:, V_LG + c:V_LG + c + 1],
                bias=vec[:, V_LB + c:V_LB + c + 1])
            P.dma('sp', S.Y[:, 4 + c, yo:yo + N], o[:, 0:N], r=[o.b], w=[dummy])


def phase2b(C):
    P, A, I, S, l = C.P, C.A, C.I, C.S, C.l
    dummy = C.dummy
    Kr = A.ring("Kh", 2, [8448], BF16)
    Vr = A.ring("Vh", 2, [66, 128], BF16)
    Qr = A.ring("Qh", 2, [NTOK], BF16)
    ptr = A.ring("pt", 6, [2, 512], BF16)
    accr = A.ring("acc", 4, [2, 512], F32)
    Ocr = A.ring("Oc", 4, [512], F32)
    om = [A.alloc(f"om{m}", [512], F32) for m in range(2)]
    lnr = A.ring("lnr", 2, [512], F32)
    rs = A.ring("rs", 2, [512], F32)
    o = A.alloc("o", [512], F32)
    osq = A.alloc("osq", [512], F32)
    rr = A.alloc("rr", [512], F32)
    tb = A.ring("tb", 2, [512], BF16)
    pss = Ring(C.ps[0:6])
    pso = [C.ps[6], C.ps[7]]
    VDv = S.VD.rearrange("(c p) v -> p c v", p=128)
    LA = 2
    pend = []

    def tail(N, yo, h, acc2, Oc, banks):
        acc = acc2[0]
        TT(P, 'dve', acc[:, :, 0:N], acc[:, :, 0:N], acc2[1][:, :, 0:N], ALU.add, r=[acc.b, acc2[1].b], w=[acc.b])
        for m in range(2):
            psu = banks[m]
            MM(P, psu, psu[:, 0:N], C.ones_f[:, :], acc[:, m, 0:N], True, True, r=[C.ones_f.b, acc.b])
            lt = lnr.next()
            ACT(P, lt[:, 0:N], psu[:, 0:N], AF.Ln, r=[psu.b], w=[lt.b])
            r1 = rs.next()
            ACT(P, r1[:, 0:N], lt[:, 0:N], AF.Exp, r=[lt.b], w=[r1.b], scale=-1.0)
            TT(P, 'dve', om[m][:, 0:N], Oc[m][:, 0:N], r1[:, 0:N], ALU.mult, r=[Oc[m].b, r1.b], w=[om[m].b])
        STT(P, 'dve', o[:, 0:N], om[1][:, 0:N], C.lam[:, 0:1], om[0][:, 0:N], ALU.mult, ALU.add,
            r=[om[0].b, om[1].b, C.lam.b], w=[o.b])
        TT(P, 'dve', osq[:, 0:N], o[:, 0:N], o[:, 0:N], ALU.mult, r=[o.b], w=[osq.b])
        psn = banks[0]
        MM(P, psn, psn[:, 0:N], C.ones_f[:, :], osq[:, 0:N], True, True, r=[C.ones_f.b, osq.b])
        TS(P, 'dve', rr[:, 0:N], psn[:, 0:N], 1.0 / 128, LN_EPS, ALU.mult, ALU.add, r=[psn.b], w=[rr.b])
        ACT(P, rr[:, 0:N], rr[:, 0:N], AF.Ln, r=[rr.b], w=[rr.b])
        ACT(P, rr[:, 0:N], rr[:, 0:N], AF.Exp, r=[rr.b], w=[rr.b], scale=-0.5)
        ob = tb.next()
        STT(P, 'dve', ob[:, 0:N], o[:, 0:N], C.gs[:, 0:1], rr[:, 0:N], ALU.mult, ALU.mult, r=[o.b, rr.b, C.gs.b],
            w=[ob.b])
        P.dma('sp', S.Y[:, 8 + h, yo:yo + N], ob[:, 0:N], r=[ob.b], w=[dummy])

    def flush(banks):
        while pend:
            tail(*pend.pop(0), banks)

    def copy_out(dst, src, N):
        P.op('dve', lambda e: e.tensor_copy(out=dst[:, 0:N], in_=src[:, 0:N]), r=[src.b], w=[dst.b])

    def acc_first(acc, pt, N):
        P.op('dve', lambda e: e.tensor_copy(out=acc[:, :, 0:N], in_=pt[:, :, 0:N]), r=[pt.b], w=[acc.b])

    def qtile(h, qt, Kh, Vh, Qh):
        N = 512 if qt < 8 else 256
        q0 = qt * 512
        chunks = list(range(66)) if qt < 8 else [0, 1]
        n = len(chunks)
        sb = {}
        acc2 = [accr.next(), accr.next()]

        def qk(ci):
            c = chunks[ci]
            banks = []
            for m in range(2):
                lo, hi = m * 64, (m + 1) * 64
                ps = pss.next()
                MM(P, ps, ps[:, 0:N], Kh[lo:hi, c * 128:(c + 1) * 128], Qh[lo:hi, q0:q0 + N], True, True,
                   r=[Kh.b, Qh.b])
                banks.append(ps)
            sb[ci] = banks

        for ci in range(min(LA, n)):
            qk(ci)
        for ci in range(n):
            if ci + LA < n:
                qk(ci + LA)
            c = chunks[ci]
            pt = ptr.next()
            for m in range(2):
                ps = sb[ci][m]
                ACT(P, pt[:, m, 0:N], ps[:, 0:N], AF.Exp, r=[ps.b], wacc=[pt.b], scale=0.125)
            cur_banks = sb[ci]
            del sb[ci]
            for m in range(2):
                MM(P, pso[m], pso[m][:, 0:N], Vh[:, c, :], pt[:, m, 0:N], ci == 0, ci == n - 1, r=[Vh.b, pt.b])
            acc = acc2[ci % 2]
            if ci < 2:
                acc_first(acc, pt, N)
            else:
                TT(P, 'dve', acc[:, :, 0:N], acc[:, :, 0:N], pt[:, :, 0:N], ALU.add, r=[acc.b, pt.b], w=[acc.b])
            if ci == 6:
                flush(cur_banks)
        flush(cur_banks)
        Oc = [Ocr.next(), Ocr.next()]
        for m in range(2):
            copy_out(Oc[m], pso[m], N)
        yo = q0 if qt < 8 else SOWN
        pend.append((N, yo, h, acc2, Oc))

    for h in range(4):
        Kh, Vh, Qh = Kr.next(), Vr.next(), Qr.next()
        P.dma('sp', Kh[:, :], S.KD[:, h, :], w=[Kh.b])
        P.dma('sp', Qh[:, :], S.QD[:, h, :], w=[Qh.b])
        for a in range(0, 66, 22):
            P.dma('sp', Vh[:, a:a + 22, :], VDv[:, a:a + 22, h * 128:(h + 1) * 128], wacc=[Vh.b])
        for qt in range(9):
            qtile(h, qt, Kh, Vh, Qh)
    flush([C.ps[0], C.ps[1]])


def phase2c(C):
    P, A, I, S, l = C.P, C.A, C.I, C.S, C.l
    dummy = C.dummy
    a16 = A.alloc("a16", [8, 128], BF16, parts=16)
    rm = A.alloc("rm", [8, 512], BF16, parts=16)
    P.dma('pool', a16[:, :, :], I.a16.rearrange("r (c p) -> r c p", c=8), w=[a16.b])
    for b in range(8):
        P.dma('pool', rm[:, b, :], I.rm[:, b * 512:(b + 1) * 512], wacc=[rm.b])
    Kr = A.ring("Kc", 2, [SOWN + 512], BF16)
    Kcr = A.ring("Kcc", 2, [CTXL], BF16)
    Qr = A.ring("Qc", 2, [NTOK], BF16)
    Vr = A.ring("Vh", 2, [36, 64], BF16)
    Vcr = A.ring("Vch", 2, [2, 64], BF16)
    Br = A.ring("bias", 2, [8, 512], BF16)
    ptr = A.ring("pt", 6, [512], BF16)
    rs = A.ring("rs", 2, [512], F32, parts=64)
    tb = A.ring("tb", 3, [512], BF16, parts=64)
    pss = Ring(C.ps[0:4])
    accs = Ring([(C.ps[4], C.ps[5]), (C.ps[6], C.ps[7])])
    VNv = S.VN.rearrange("(c p) v -> p c v", p=128)
    VNCv = S.VNC.rearrange("(c p) v -> p c v", p=128)
    Kc = Kcc = Qc = None
    for h in range(8):
        hc, po = h // 2, (h % 2) * 64
        if h % 2 == 0:
            Kc, Kcc, Qc = Kr.next(), Kcr.next(), Qr.next()
            P.dma('sp', Kc[:, :], S.KN[:, hc, :], w=[Kc.b])
            P.dma('sp', Kcc[:, :], S.KNC[:, hc, :], w=[Kcc.b])
            P.dma('sp', Qc[:, :], S.QN[:, hc, :], w=[Qc.b])
        Vh, Vch, Bh = Vr.next(), Vcr.next(), Br.next()
        for a in range(0, 36, 12):
            P.dma('sp', Vh[:, a:a + 12, :], VNv[:, a:a + 12, h * 64:(h + 1) * 64], wacc=[Vh.b])
        P.dma('sp', Vch[:, :, :], VNCv[:, :, h * 64:(h + 1) * 64], w=[Vch.b])
        P.pool(lambda e, Bh=Bh: e.memset(Bh[:, :, :], 0.0), w=[Bh.b])
        for rq in range(8):
            for par in range(2):
                clo = max(0, -((-(rq - 3 - par)) // 2))
                chi = min(7, (rq + 11 - par) // 2)
                n = chi - clo + 1
                d0 = 2 * clo + par - rq + 3
                src = S.RPBF[h, d0:d0 + 2 * n - 1:2, :, :].rearrange("i k q -> k i q")
                P.dma('sp', Bh[par * 64:(par + 1) * 64, clo:chi + 1, rq * 64:(rq + 1) * 64], src, wacc=[Bh.b])
        for blk in range(9):
            N = 512 if blk < 8 else 256
            q0 = blk * 512
            pso, psu = accs.next()
            nch = 10 if blk < 8 else 2
            sb = {}

            def qk(ci):
                ps = pss.next()
                if blk < 8 and ci < 8:
                    k0 = blk * 512 + ci * 128
                    MM(P, ps, ps[:, 0:N], Kc[po:po + 64, k0:k0 + 128], Qc[po:po + 64, q0:q0 + N], True, False,
                       r=[Kc.b, Qc.b])
                    MM(P, ps, ps[:, 0:N], C.ident8[:, :], Bh[:, ci, :], False, False, r=[C.ident8.b, Bh.b])
                    MM(P, ps, ps[:, 0:N], a16[:, ci, :], rm[:, blk, :], False, True, r=[a16.b, rm.b])
                else:
                    cc = ci - 8 if blk < 8 else ci
                    MM(P, ps, ps[:, 0:N], Kcc[po:po + 64, cc * 128:(cc + 1) * 128], Qc[po:po + 64, q0:q0 + N], True, True,
                       r=[Kcc.b, Qc.b])
                sb[ci] = ps

            LA = 2
            for ci in range(min(LA, nch)):
                qk(ci)
            for ci in range(nch):
                if ci + LA < nch:
                    qk(ci + LA)
                ps = sb.pop(ci)
                if blk < 8 and ci < 8:
                    vl = Vh[:, blk * 4 + ci, :]
                    vb = Vh.b
                else:
                    cc = ci - 8 if blk < 8 else ci
                    vl = Vch[:, cc, :]
                    vb = Vch.b
                pt = ptr.next()
                ACT(P, pt[:, 0:N], ps[:, 0:N], AF.Exp, r=[ps.b], w=[pt.b], scale=0.125)
                MM(P, pso, pso[0:64, 0:N], vl, pt[:, 0:N], ci == 0, ci == nch - 1, r=[vb, pt.b])
                MM(P, psu, psu[0:64, 0:N], C.ones_bf[:, 0:64], pt[:, 0:N], ci == 0, ci == nch - 1, r=[C.ones_bf.b, pt.b])
            r1 = rs.next()
            P.dve(lambda e, r1=r1, psu=psu, N=N: e.reciprocal(out=r1[:, 0:N], in_=psu[0:64, 0:N]), r=[psu.b], w=[r1.b])
            ob = tb.next()
            TT(P, 'dve', ob[:, 0:N], pso[0:64, 0:N], r1[:, 0:N], ALU.mult, r=[pso.b, r1.b], w=[ob.b])
            yo = q0 if blk < 8 else SOWN
            P.dma('sp', S.Y[po:po + 64, 12 + hc, yo:yo + N], ob[:, 0:N], r=[ob.b], w=[dummy])


def resid_ln(C, P, pbs, xres, gb, lng, lnb, dst_rows, tmp_r, tmp_t, st6, mv, rsd, final, wbuf=None):
    t, r = tmp_t, tmp_r
    for hf in range(2):
        TT(P, 'dve', t[:, hf * 512:(hf + 1) * 512], pbs[hf][:, :], gb[:, hf * 512:(hf + 1) * 512], ALU.mult,
           r=[pbs[hf].b, gb.b], wacc=[t.b])
    STT(P, 'dve', r[:, :], xres, ALPHA, t[:, :], ALU.mult, ALU.add, r=[t.b] + xres_b(C), w=[r.b])
    for hf in range(2):
        P.dve(lambda e, hf=hf: e.bn_stats(out=st6[:, hf, :], in_=r[:, hf * 512:(hf + 1) * 512]), r=[r.b], wacc=[st6.b])
    P.dve(lambda e: e.bn_aggr(out=mv[:, :], in_=st6[:, :, :]), r=[st6.b], w=[mv.b])
    TS(P, 'dve', rsd[:, :], mv[:, 1:2], LN_EPS, None, ALU.add, None, r=[mv.b], w=[rsd.b])
    ACT(P, rsd[:, :], rsd[:, :], AF.Sqrt, r=[rsd.b], w=[rsd.b])
    P.dve(lambda e: e.reciprocal(out=rsd[:, :], in_=rsd[:, :]), r=[rsd.b], w=[rsd.b])
    TS(P, 'dve', r[:, :], r[:, :], mv[:, 0:1], rsd[:, 0:1], ALU.subtract, ALU.mult, r=[r.b, mv.b, rsd.b], w=[r.b])
    TT(P, 'dve', r[:, :], r[:, :], lng[:, :], ALU.mult, r=[r.b, lng.b], w=[r.b])
    TT(P, 'dve', r[:, :], r[:, :], lnb[:, :], ALU.add, r=[r.b, lnb.b], w=[r.b])
    if wbuf is None:
        P.dma('sp', dst_rows, r[:, :], r=[r.b], w=[C.dummy], final=final)
    else:
        P.dma('sp', dst_rows, r[:, :], r=[r.b], wacc=[wbuf], final=final)


def xres_b(C):
    return [C._xres_b]


def phase3(C):
    P, A, I, S, l = C.P, C.A, C.I, C.S, C.l
    vec = C.vec
    Wg = A.alloc("Wg", [8, 4096], BF16)
    Wb = A.alloc("Wb", [16, 1024], BF16)
    Wo = A.alloc("Wo", [8, 1024], BF16)
    load_w(P, Wg, I.w_in[l], 8, 5632, 9728, Wg.b)
    load_w(P, Wb, I.w_branch[l], 16, 0, 1024, Wb.b)
    load_w(P, Wo, I.w_o[l], 8, 0, 1024, Wo.b)
    gb = A.alloc("gb", [2, 1024], F32)
    P.dma('sp', gb[:, :, :], S.GB[0:2].rearrange("g p d -> p g d"), w=[gb.b])
    lng = A.alloc("lng", [1024], F32)
    lnb = A.alloc("lnb", [1024], F32)
    P.dma('sp', lng[:, :], bcast_row(C, R_LNG0, 1024), w=[lng.b])
    P.dma('sp', lnb[:, :], bcast_row(C, R_LNB0, 1024), w=[lnb.b])
    uTr = A.ring("uT", 2, [8, 256], BF16)
    Yr = A.ring("Yt", 2, [16, 256], BF16)
    xtr = A.ring("xt", 2, [2, 1024], F32)
    mT = A.alloc("mT", [8, 256], BF16)
    sg = A.ring("sg", 3, [256], F32)
    tm_ = A.ring("tm", 3, [256], F32)
    accr = A.ring("acc", 2, [256], F32)
    tr_ = A.ring("tr", 2, [1024], F32)
    tt_ = A.ring("tt", 2, [1024], F32)
    st6 = A.alloc("st6", [2, 6], F32)
    mv = A.alloc("mv", [2], F32)
    rsd = A.alloc("rsd", [1], F32)
    psr = Ring(C.ps[0:4])
    pso = Ring([(C.ps[4], C.ps[5]), (C.ps[6], C.ps[7])])
    N = 256
    for ti in range(NTOK // N):
        t0 = ti * N
        isctx = t0 >= SOWN
        j = 1 if isctx else 0
        uT, Yt, xt = uTr.next(), Yr.next(), xtr.next()
        P.dma('sp', uT[:, :, :], S.UT[:, :, t0:t0 + N], w=[uT.b])
        P.dma('sp', Yt[:, :, :], S.Y[:, :, t0:t0 + N], w=[Yt.b])
        xsrc = C.xin_ctx[t0 - SOWN:t0 - SOWN + N, :] if isctx else C.own_rows(t0, N)
        P.dma('sp', xt[:, :, :], xsrc.rearrange("(j p) f -> p j f", p=128), r=([] if isctx else [C.ain_b[t0 // 512]]), w=[xt.b])
        for oc in range(8):
            acc = accr.next()
            for i in range(4):
                ps = psr.next()
                gcol = i * 1024 + oc * 128
                for k in range(8):
                    MM(P, ps, ps[:, 0:N], Wg[:, k, gcol:gcol + 128], uT[:, k, :], k == 0, k == 7, r=[Wg.b, uT.b])
                g = sg.next()
                gch = 44 + i * 8 + oc
                ACT(P, g[:, :], ps[:, 0:N], AF.Sigmoid, r=[ps.b, vec.b], w=[g.b], bias=vec[:, V_BIN + gch:V_BIN + gch + 1])
                ps2 = psr.next()
                for k in range(4):
                    MM(P, ps2, ps2[:, 0:N], Wb[:, i * 4 + k, oc * 128:(oc + 1) * 128], Yt[:, i * 4 + k, :], k == 0, k == 3,
                       r=[Wb.b, Yt.b])
                bbc = V_BB + i * 8 + oc
                if i == 0:
                    STT(P, 'dve', acc[:, :], ps2[:, 0:N], vec[:, bbc:bbc + 1], g[:, :], ALU.add, ALU.mult,
                        r=[ps2.b, g.b, vec.b], w=[acc.b])
                else:
                    tmv = tm_.next()
                    STT(P, 'dve', tmv[:, :], ps2[:, 0:N], vec[:, bbc:bbc + 1], g[:, :], ALU.add, ALU.mult,
                        r=[ps2.b, g.b, vec.b], w=[tmv.b])
                    if i < 3:
                        TT(P, 'dve', acc[:, :], acc[:, :], tmv[:, :], ALU.add, r=[acc.b, tmv.b], w=[acc.b])
                    else:
                        TT(P, 'dve', mT[:, oc, :], acc[:, :], tmv[:, :], ALU.add, r=[acc.b, tmv.b], wacc=[mT.b])
        for jj in range(2):
            pb = pso.next()
            for hf in range(2):
                for k in range(8):
                    MM(P, pb[hf], pb[hf][:, :], mT[:, k, jj * 128:(jj + 1) * 128], Wo[:, k, hf * 512:(hf + 1) * 512],
                       k == 0, k == 7, r=[mT.b, Wo.b])
            C._xres_b = xt.b
            resid_ln(C, P, pb, xt[:, jj, :], T(gb[:, j, :], gb.b), lng, lnb,
                     S.X1[t0 + jj * 128:t0 + (jj + 1) * 128, :], tr_.next(), tt_.next(), st6, mv, rsd, False)


def phase4(C):
    P, A, I, S, l = C.P, C.A, C.I, C.S, C.l
    W1 = A.alloc("W1", [8, 4096], BF16)
    W2 = A.alloc("W2", [32, 1024], BF16)
    load_w(P, W1, I.w_ff1[l], 8, 0, 4096, W1.b)
    load_w(P, W2, I.w_ff2[l], 32, 0, 1024, W2.b)
    gb = A.alloc("gb", [2, 1024], F32)
    P.dma('sp', gb[:, :, :], S.GB[2:4].rearrange("g p d -> p g d"), w=[gb.b])
    lng = A.alloc("lng", [1024], F32)
    lnb = A.alloc("lnb", [1024], F32)
    P.dma('sp', lng[:, :], bcast_row(C, R_LNG1, 1024), w=[lng.b])
    P.dma('sp', lnb[:, :], bcast_row(C, R_LNB1, 1024), w=[lnb.b])
    xtr = A.ring("xt", 2, [2, 1024], F32)
    uT = A.alloc("u2T", [8, 256], BF16)
    hT = A.alloc("hT", [32, 256], BF16)
    sq = A.ring("sq", 3, [256], F32)
    tr_ = A.ring("tr", 2, [1024], F32)
    tt_ = A.alloc("tt", [1024], F32)
    st6 = A.alloc("st6", [2, 6], F32)
    mv = A.alloc("mv", [2], F32)
    rsd = A.alloc("rsd", [1], F32)
    psr = Ring(C.ps[0:4])
    pso = Ring([(C.ps[4], C.ps[5]), (C.ps[6], C.ps[7])])
    N = 256
    for ti in range(NTOK // N):
        t0 = ti * N
        isctx = t0 >= SOWN
        j = 1 if isctx else 0
        xt = xtr.next()
        P.dma('sp', xt[:, :, :], S.X1[t0:t0 + N, :].rearrange("(j p) f -> p j f", p=128), w=[xt.b])
        for k in range(8):
            ps = psr.next()
            for jj in range(2):
                TR(P, ps, ps[:, jj * 128:(jj + 1) * 128], xt[:, jj, k * 128:(k + 1) * 128], C.ident_f[:, :],
                   r=[xt.b, C.ident_f.b])
            ACT(P, uT[:, k, :], ps[:, 0:N], AF.Identity, r=[ps.b, C.modfm.b], wacc=[uT.b],
                scale=C.modfm[:, 32 + k, j:j + 1], bias=C.modfm[:, 24 + k, j:j + 1])
        for fc in range(32):
            ps = psr.next()
            for k in range(8):
                MM(P, ps, ps[:, 0:N], W1[:, k, fc * 128:(fc + 1) * 128], uT[:, k, :], k == 0, k == 7, r=[W1.b, uT.b])
            s = sq.next()
            ACT(P, s[:, :], ps[:, 0:N], AF.Square, r=[ps.b], w=[s.b])
            STT(P, 'dve', hT[:, fc, :], ps[:, 0:N], 0.0, s[:, :], ALU.is_gt, ALU.mult, r=[ps.b, s.b], wacc=[hT.b])
        for jj in range(2):
            pb = pso.next()
            for hf in range(2):
                for fc in range(32):
                    MM(P, pb[hf], pb[hf][:, :], hT[:, fc, jj * 128:(jj + 1) * 128], W2[:, fc, hf * 512:(hf + 1) * 512],
                       fc == 0, fc == 31, r=[hT.b, W2.b])
            C._xres_b = xt.b
            if isctx:
                dst = C.xo_ctx[t0 - SOWN + jj * 128:t0 - SOWN + (jj + 1) * 128, :]
            else:
                dst = C.out_rows(t0 + jj * 128, 128)
            resid_ln(C, P, pb, xt[:, jj, :], T(gb[:, j, :], gb.b), lng, lnb, dst, tr_.next(), tt_, st6, mv, rsd, C.final,
                     wbuf=(None if (isctx or C.final) else C.ain_b[t0 // 512]))
        if (not C.final) and (not isctx) and (t0 % 512 == 256):
            i = t0 // 512
            P.op('pool', lambda e, i=i: e.collective_compute("AllGather", ALU.bypass, replica_groups=[[0, 1], [2, 3], [4, 5], [6, 7]],
                                                           ins=[S.AIN[i].ap().opt()], outs=[S.STG[i].ap().opt()]),
                 r=[C.ain_b[i]], w=[C.stg_b[i]], cc=True)


_PROG_CACHE = {}


def _consts():
    GRID_W = 64
    nf = 16
    freqs = np.power(10000.0, -np.arange(nf, dtype=np.float32) / nf).astype(np.float32)
    t = np.arange(8192)
    pos = np.stack([t // GRID_W, t % GRID_W], axis=-1).astype(np.float32)
    ang = pos[:, :, None] * freqs
    cos, sin = np.cos(ang).astype(np.float32), np.sin(ang).astype(np.float32)
    p = np.arange(128)
    axis = (p // 32) % 2
    half = (p // 16) % 2
    f = p % 16
    COS = cos[:, axis, f].T
    SIN = sin[:, axis, f].T * np.where(half == 0, -1.0, 1.0)[:, None]
    rope = np.stack([COS, SIN], axis=1).astype(np.float32)
    perm = np.zeros((128, 128), np.float32)
    perm[p ^ 16, p] = 1.0
    kc = np.arange(64)[:, None]
    qc = np.arange(64)[None, :]
    cstart = np.clip(qc - 8, 0, 48)
    col_ok = (kc >= cstart) & (kc < cstart + 16)
    oh = np.zeros((32, 64, 64), np.float32)
    dc = np.clip(kc - qc + 15, 0, 30)
    for jx in range(31):
        oh[jx] = (dc == jx)
    oh[31] = ~col_ok
    oh = oh.reshape(32, 4096)
    a16 = np.zeros((16, 8, 128), np.float32)
    for c in range(8):
        for pp in range(128):
            a16[2 * c + pp // 64, c, pp] = 1.0
    a16 = a16.reshape(16, 1024)
    rms = []
    for hf in range(2):
        rm = np.zeros((16, 8, 8, 64), np.float32)
        for blk in range(8):
            q0g = hf * 64 + blk * 8
            for rq in range(8):
                qr = q0g + rq
                r0 = min(max(qr - 4, 0), 120)
                for rk in range(16):
                    kr = q0g - 4 + rk
                    ok = (r0 <= kr < r0 + 8)
                    if not ok:
                        rm[rk, blk, rq, :] = NEG
        rms.append(rm.reshape(16, 8 * 512))
    flags = []
    for hf in range(2):
        fl = np.zeros((128, 2), np.float32)
        fl[:, 0] = 1.0 if hf == 1 else 0.0
        fl[:, 1] = 1.0 if hf == 0 else 0.0
        flags.append(fl)
    return dict(rope=rope, perm=perm, oh=oh, a16=a16, rms=rms, flags=flags)


def _fm(v, nch):
    return np.ascontiguousarray(np.asarray(v, np.float32).reshape(nch, 128).T)


def _layer_maps(inp, l, depth=4):
    import math
    vec = np.zeros((128, NVEC), np.float32)
    vec[:, V_BIN:V_BIN + 76] = _fm(inp['b_in'][l], 76)
    vec[:, V_BADA:V_BADA + 48] = _fm(inp['b_ada'][l], 48)
    cw = inp['conv_a_w'][l]
    for k in range(3):
        vec[:, V_CW + k * 4:V_CW + k * 4 + 4] = _fm(cw[k], 4)
    vec[:, V_CB:V_CB + 4] = _fm(inp['conv_a_b'][l], 4)
    dw = inp['conf_dw_w'][l]
    for k in range(31):
        vec[:, V_DW + k * 4:V_DW + k * 4 + 4] = _fm(dw[k], 4)
    vec[:, V_DB:V_DB + 4] = _fm(inp['conf_dw_b'][l], 4)
    vec[:, V_LG:V_LG + 4] = _fm(inp['conf_ln_g'][l], 4)
    vec[:, V_LB:V_LB + 4] = _fm(inp['conf_ln_b'][l], 4)
    vec[:, V_DG] = inp['diff_norm_g'][l]
    for i in range(4):
        vec[:, V_BB + i * 8:V_BB + i * 8 + 8] = _fm(inp['b_branch'][l, i], 8)
    lam_init = 0.8 - 0.6 * math.exp(-0.3 * l)
    vec[:, V_LAM] = lam_init
    vec[:, V_LAM + 1] = 1.0 - lam_init
    rows = np.zeros((1, NROWS), np.float32)
    b_in = inp['b_in'][l]
    b_ada = inp['b_ada'][l]
    rows[0, R_BDV:R_BDV + 512] = b_in[3584:4096]
    rows[0, R_BNV:R_BNV + 512] = b_in[5120:5632]
    rows[0, R_BG1:R_BG1 + 1024] = b_ada[2048:3072]
    rows[0, R_BG2:R_BG2 + 1024] = b_ada[5120:6144]
    rows[0, R_LNG0:R_LNG0 + 1024] = inp['ln_g'][l, 0]
    rows[0, R_LNB0:R_LNB0 + 1024] = inp['ln_b'][l, 0]
    rows[0, R_LNG1:R_LNG1 + 1024] = inp['ln_g'][l, 1]
    rows[0, R_LNB1:R_LNB1 + 1024] = inp['ln_b'][l, 1]
    rows[0, R_LAM:R_LAM + 256] = inp['diff_lambda'][l].reshape(-1)
    rpbT = np.full((32, 120), NEG, np.float32)
    rpbT[0:31, :] = inp['na_rpb'][l].reshape(120, 31).T
    return dict(
        w_ada=inp['w_ada'][l:l + 1], w_in=inp['w_in'][l:l + 1], vec=vec[None], rows=rows[None], rpbT=rpbT[None],
        w_branch=np.ascontiguousarray(inp['w_branch'][l].reshape(1, 2048, DM)), w_o=inp['w_o'][l:l + 1],
        w_ff1=inp['w_ff1'][l:l + 1], w_ff2=inp['w_ff2'][l:l + 1])


def _core_maps(inp, K, x, xc, lws):
    in_maps = []
    for core in range(8):
        b, hf = core // 2, core % 2
        m = dict(lws)
        m['x_own'] = np.ascontiguousarray(x[b, hf * SOWN:(hf + 1) * SOWN])
        m['x_seq'] = np.ascontiguousarray(x[b])
        m['xc'] = np.ascontiguousarray(xc[b])
        cc = np.stack([inp['c'][b], inp['c_ctx']], axis=-1).astype(np.float32)
        m['cc'] = np.ascontiguousarray(cc.reshape(8, 128, 2).transpose(1, 0, 2))
        m['rope_own'] = np.ascontiguousarray(K['rope'][:, :, hf * SOWN:(hf + 1) * SOWN])
        m['rope_seq'] = K['rope']
        m['halo_flag'] = K['flags'][hf]
        m['perm'] = K['perm']
        m['oh'] = K['oh']
        m['a16'] = K['a16']
        m['rm'] = K['rms'][hf]
        in_maps.append(m)
    return in_maps


def kernel(**inp):
    inp = {k: np.asarray(v) for k, v in inp.items()}
    x = np.ascontiguousarray(inp['x'], dtype=np.float32)
    xc = np.ascontiguousarray(inp['ctx'], dtype=np.float32)
    K = _consts()
    depth = 4
    if 'p4' not in _PROG_CACHE:
        _PROG_CACHE['p4'] = build_program(depth, True)
    nc = _PROG_CACHE['p4']
    per = [_layer_maps(inp, l) for l in range(depth)]
    lws = {k: np.ascontiguousarray(np.concatenate([p[k] for p in per], axis=0)) for k in per[0]}
    in_maps = _core_maps(inp, K, x, xc, lws)
    res = run_bass_kernel_spmd(nc, in_maps, core_ids=list(range(8)))
    out = np.empty_like(x)
    for core in range(8):
        b, hf = core // 2, core % 2
        out[b, hf * SOWN:(hf + 1) * SOWN] = res.results[core]['x_out']
    return out
```
